# Optimizing a Trainium2 kernel written in Bass

```python
import math
import jax, jax.numpy as jnp
from jax import lax
import numpy as np

D_MODEL = 1024
BATCH = 8
SEQ = 4096
DEPTH = 2

GRID_W = 64
CTX_LEN = 256
EPS = 1e-6

D_LRU = D_MODEL // 4
LRU_HEADS = 4
LRU_HEAD_DIM = D_LRU // LRU_HEADS
CONV_W = 4
LRU_C = 8.0
MLA_V = 64
MLA_NOPE = 64
MLA_ROPE = 32
MLA_QK = MLA_NOPE + MLA_ROPE
D_MLA = D_MODEL // 2
MLA_HEADS = D_MLA // MLA_V
Q_RANK = 3 * D_MODEL // 8
KV_RANK = D_MODEL // 4
ROPE_FREQS = MLA_ROPE // 4
ROPE_BASE = 10000.0
D_POOL = D_MODEL // 4
POOL_WINDOWS = (2, 4, 8, 16)
POOL_GROUPS = len(POOL_WINDOWS)
POOL_GROUP_DIM = D_POOL // POOL_GROUPS
D_MIX = D_LRU + D_MLA + D_POOL
IN_SPLITS = (D_LRU, 2 * D_LRU, 2 * D_LRU + Q_RANK, 2 * D_LRU + Q_RANK + KV_RANK,
             2 * D_LRU + Q_RANK + KV_RANK + MLA_ROPE)
D_IN = IN_SPLITS[-1] + D_POOL
D_FF = 4 * D_MODEL
Q_BLOCK = 128

kernel_name = "hybrid_rglru_mla_pool_diffusion_block"

F32 = jnp.float32


def _rmsnorm(x, g):
    x32 = x.astype(F32)
    y = x32 * lax.rsqrt(jnp.mean(x32 * x32, axis=-1, keepdims=True) + EPS)
    return (y * g.astype(F32)).astype(x.dtype)


def _modulate(h, shift, scale):
    return h * (1.0 + scale) + shift


def _axial_rope_tables(L):
    rows = L // GRID_W
    row = jnp.repeat(jnp.arange(rows, dtype=F32), GRID_W)
    col = jnp.tile(jnp.arange(GRID_W, dtype=F32), rows)
    freqs = jnp.power(ROPE_BASE, -jnp.arange(ROPE_FREQS, dtype=F32) / ROPE_FREQS)
    ang = jnp.stack([row, col], axis=-1)[:, :, None] * freqs
    return jnp.cos(ang), jnp.sin(ang)


def _apply_axial_rope(x, cos, sin):
    B, L, H, _ = x.shape
    xs = x.astype(F32).reshape(B, L, H, 2, 2, ROPE_FREQS)
    x1, x2 = xs[..., 0, :], xs[..., 1, :]
    c = cos[None, :, None]
    s = sin[None, :, None]
    y = jnp.stack([x1 * c - x2 * s, x2 * c + x1 * s], axis=-2)
    return y.reshape(B, L, H, MLA_ROPE).astype(x.dtype)


def _dwconv(x, w, b):
    L = x.shape[1]
    left = (CONV_W - 1) // 2
    xp = jnp.pad(x, ((0, 0), (left, CONV_W - 1 - left), (0, 0)))
    y = sum(xp[:, k:k + L] * w[k] for k in range(CONV_W))
    return y + b


def _rglru_coeffs(xc, w_a, b_a, w_x, b_x, lam):
    B, L, _ = xc.shape
    xh = xc.reshape(B, L, LRU_HEADS, LRU_HEAD_DIM)
    r = jax.nn.sigmoid((jnp.einsum('blhi,hij->blhj', xh, w_a).reshape(B, L, D_LRU) + b_a).astype(F32))
    i = jax.nn.sigmoid((jnp.einsum('blhi,hij->blhj', xh, w_x).reshape(B, L, D_LRU) + b_x).astype(F32))
    log_a = -LRU_C * r * jax.nn.softplus(-lam.astype(F32))
    a = jnp.exp(log_a)
    b = jnp.sqrt(-jnp.expm1(2.0 * log_a)) * (i * xc.astype(F32))
    return a, b


def _linear_scan(a, b, h0):
    def combine(e1, e2):
        a1, b1 = e1
        a2, b2 = e2
        return a1 * a2, a2 * b1 + b2
    A, H = lax.associative_scan(combine, (a, b), axis=1)
    return H + A * h0[:, None, :]


def _bidir_rglru(xc_lat, xc_ctx, w_a, b_a, w_x, b_x, lam):
    B = xc_lat.shape[0]
    zeros = jnp.zeros((B, D_LRU), F32)
    y_lat = 0.0
    y_ctx = 0.0
    for d in range(2):
        xl = xc_lat if d == 0 else jnp.flip(xc_lat, axis=1)
        xcx = xc_ctx if d == 0 else jnp.flip(xc_ctx, axis=1)
        a_c, b_c = _rglru_coeffs(xcx, w_a[d], b_a[d], w_x[d], b_x[d], lam[d])
        a_l, b_l = _rglru_coeffs(xl, w_a[d], b_a[d], w_x[d], b_x[d], lam[d])
        h_c = _linear_scan(a_c, b_c, zeros)
        h_l = _linear_scan(a_l, b_l, h_c[:, -1])
        if d == 1:
            h_c = jnp.flip(h_c, axis=1)
            h_l = jnp.flip(h_l, axis=1)
        y_lat = y_lat + h_l
        y_ctx = y_ctx + h_c
    return y_lat.astype(xc_lat.dtype), y_ctx.astype(xc_ctx.dtype)


def _mla_q(q_lat, g_q_lat, w_uq, g_qn, rope):
    B, L, _ = q_lat.shape
    q = (_rmsnorm(q_lat, g_q_lat) @ w_uq).reshape(B, L, MLA_HEADS, MLA_QK)
    q = _rmsnorm(q, g_qn)
    if rope is not None:
        q = jnp.concatenate([q[..., :MLA_NOPE], _apply_axial_rope(q[..., MLA_NOPE:], *rope)], axis=-1)
    return q


def _mla_kv(kv_lat, k_rope, g_kv_lat, w_ukv, g_kn, rope):
    B, L, _ = kv_lat.shape
    kv = (_rmsnorm(kv_lat, g_kv_lat) @ w_ukv).reshape(B, L, MLA_HEADS, MLA_NOPE + MLA_V)
    k_nope, v = kv[..., :MLA_NOPE], kv[..., MLA_NOPE:]
    k_r = jnp.broadcast_to(k_rope[:, :, None, :], (B, L, MLA_HEADS, MLA_ROPE))
    k = _rmsnorm(jnp.concatenate([k_nope, k_r], axis=-1), g_kn)
    if rope is not None:
        k = jnp.concatenate([k[..., :MLA_NOPE], _apply_axial_rope(k[..., MLA_NOPE:], *rope)], axis=-1)
    return k, v


def _block_attention(q, k, v):
    B, L, H, Dq = q.shape
    nb = L // Q_BLOCK
    qb = jnp.moveaxis(q.reshape(B, nb, Q_BLOCK, H, Dq), 1, 0)
    scale = 1.0 / math.sqrt(Dq)

    def attend(qi):
        s = jnp.einsum('bqhd,bkhd->bhqk', qi, k).astype(F32) * scale
        p = jax.nn.softmax(s, axis=-1)
        return jnp.einsum('bhqk,bkhd->bqhd', p.astype(v.dtype), v)

    o = lax.map(attend, qb)
    return jnp.moveaxis(o, 0, 1).reshape(B, L, H * v.shape[-1])


def _pool_mixer(u, w_pool, pool_scale):
    B, L, _ = u.shape
    ug = u.astype(F32).reshape(B, L, POOL_GROUPS, POOL_GROUP_DIM)
    csum = jnp.concatenate([jnp.zeros((B, 1, POOL_GROUPS, POOL_GROUP_DIM), F32),
                            jnp.cumsum(ug, axis=1)], axis=1)
    t = jnp.arange(L)
    means = []
    for g, w in enumerate(POOL_WINDOWS):
        lo = jnp.clip(t - w // 2, 0, L)
        hi = jnp.clip(t - w // 2 + w, 0, L)
        cnt = (hi - lo).astype(F32)[None, :, None]
        means.append((csum[:, hi, g] - csum[:, lo, g]) / cnt)
    mixed = jnp.stack(means, axis=2) - ug
    y = jnp.einsum('blgi,gij->blgj', mixed, w_pool.astype(F32)).reshape(B, L, D_POOL)
    return (y * pool_scale.astype(F32)).astype(u.dtype)


def _mixer(h_lat, h_ctx, rope, need_ctx_out, w_in, conv_w, conv_b, lru_w_a, lru_b_a, lru_w_x, lru_b_x,
           lru_lambda, g_q_lat, w_uq, g_kv_lat, w_ukv, g_qn, g_kn, w_pool, pool_scale, w_out):
    z_lat = h_lat @ w_in
    z_ctx = h_ctx @ w_in
    lx_l, lg_l, ql_l, kvl_l, kr_l, pu_l = jnp.split(z_lat, IN_SPLITS, axis=-1)
    lx_c, lg_c, ql_c, kvl_c, kr_c, pu_c = jnp.split(z_ctx, IN_SPLITS, axis=-1)
    r_l, r_c = _bidir_rglru(_dwconv(lx_l, conv_w, conv_b), _dwconv(lx_c, conv_w, conv_b),
                            lru_w_a, lru_b_a, lru_w_x, lru_b_x, lru_lambda)
    a_l = r_l * jax.nn.gelu(lg_l)
    k_c, v_c = _mla_kv(kvl_c, kr_c, g_kv_lat, w_ukv, g_kn, None)
    k_l, v_l = _mla_kv(kvl_l, kr_l[:, :, :], g_kv_lat, w_ukv, g_kn, rope)
    q_l = _mla_q(ql_l, g_q_lat, w_uq, g_qn, rope)
    att_l = _block_attention(q_l, jnp.concatenate([k_c, k_l], axis=1), jnp.concatenate([v_c, v_l], axis=1))
    p_l = _pool_mixer(pu_l, w_pool, pool_scale)
    y_lat = jnp.concatenate([a_l, att_l, p_l], axis=-1) @ w_out
    if not need_ctx_out:
        return y_lat, None
    a_c = r_c * jax.nn.gelu(lg_c)
    q_c = _mla_q(ql_c, g_q_lat, w_uq, g_qn, None)
    att_c = _block_attention(q_c, k_c, v_c)
    p_c = _pool_mixer(pu_c, w_pool, pool_scale)
    y_ctx = jnp.concatenate([a_c, att_c, p_c], axis=-1) @ w_out
    return y_lat, y_ctx


def _mlp(h, w1, w2):
    return jnp.square(jax.nn.relu(h @ w1)) @ w2


def setup_inputs(seed: int = 0) -> dict:
    key = jax.random.key(seed)
    ks = jax.random.split(key, 32)
    nrm = lambda k, shape, s: jax.random.normal(k, shape, F32) * s
    a0 = jax.random.uniform(ks[12], (DEPTH, 2, D_LRU), F32, minval=0.9, maxval=0.999)
    s0 = a0 ** (1.0 / LRU_C)
    return {
        "x": nrm(ks[0], (BATCH, SEQ, D_MODEL), 1.0),
        "c": nrm(ks[1], (BATCH, D_MODEL), 1.0),
        "ctx": nrm(ks[2], (BATCH, CTX_LEN, D_MODEL), 1.0),
        "c_ctx": nrm(ks[3], (D_MODEL,), 1.0),
        "w_mod": nrm(ks[4], (DEPTH, D_MODEL, 6 * D_MODEL), 0.5 * D_MODEL ** -0.5),
        "b_mod": nrm(ks[5], (DEPTH, 6 * D_MODEL), 0.02),
        "g_norm1": 1.0 + nrm(ks[6], (DEPTH, D_MODEL), 0.02),
        "g_norm2": 1.0 + nrm(ks[7], (DEPTH, D_MODEL), 0.02),
        "w_in": nrm(ks[8], (DEPTH, D_MODEL, D_IN), D_MODEL ** -0.5),
        "conv_w": nrm(ks[9], (DEPTH, CONV_W, D_LRU), CONV_W ** -0.5),
        "conv_b": nrm(ks[10], (DEPTH, D_LRU), 0.02),
        "lru_w_a": nrm(ks[11], (DEPTH, 2, LRU_HEADS, LRU_HEAD_DIM, LRU_HEAD_DIM), LRU_HEAD_DIM ** -0.5),
        "lru_b_a": nrm(ks[13], (DEPTH, 2, D_LRU), 0.02),
        "lru_w_x": nrm(ks[14], (DEPTH, 2, LRU_HEADS, LRU_HEAD_DIM, LRU_HEAD_DIM), LRU_HEAD_DIM ** -0.5),
        "lru_b_x": nrm(ks[15], (DEPTH, 2, D_LRU), 0.02),
        "lru_lambda": jnp.log(s0) - jnp.log1p(-s0),
        "g_q_lat": 1.0 + nrm(ks[16], (DEPTH, Q_RANK), 0.02),
        "w_uq": nrm(ks[17], (DEPTH, Q_RANK, MLA_HEADS * MLA_QK), Q_RANK ** -0.5),
        "g_kv_lat": 1.0 + nrm(ks[18], (DEPTH, KV_RANK), 0.02),
        "w_ukv": nrm(ks[19], (DEPTH, KV_RANK, MLA_HEADS * (MLA_NOPE + MLA_V)), KV_RANK ** -0.5),
        "g_qn": 1.0 + nrm(ks[20], (DEPTH, MLA_QK), 0.02),
        "g_kn": 1.0 + nrm(ks[21], (DEPTH, MLA_QK), 0.02),
        "w_pool": nrm(ks[22], (DEPTH, POOL_GROUPS, POOL_GROUP_DIM, POOL_GROUP_DIM), POOL_GROUP_DIM ** -0.5),
        "pool_scale": 1.0 + nrm(ks[23], (DEPTH, D_POOL), 0.1),
        "w_out": nrm(ks[24], (DEPTH, D_MIX, D_MODEL), D_MIX ** -0.5),
        "w_ff1": nrm(ks[25], (DEPTH, D_MODEL, D_FF), D_MODEL ** -0.5),
        "w_ff2": nrm(ks[26], (DEPTH, D_FF, D_MODEL), D_FF ** -0.5),
    }


def reference(x, c, ctx, c_ctx, w_mod, b_mod, g_norm1, g_norm2, w_in, conv_w, conv_b, lru_w_a, lru_b_a,
              lru_w_x, lru_b_x, lru_lambda, g_q_lat, w_uq, g_kv_lat, w_ukv, g_qn, g_kn, w_pool, pool_scale,
              w_out, w_ff1, w_ff2):
    L = x.shape[1]
    rope = _axial_rope_tables(L)
    c_act = jax.nn.silu(c)
    cc_act = jax.nn.silu(c_ctx)
    h = ctx
    for l in range(DEPTH):
        last = l == DEPTH - 1
        mod_l = (c_act @ w_mod[l] + b_mod[l])[:, None, :]
        mod_c = (cc_act @ w_mod[l] + b_mod[l])[None, None, :]
        sh1, sc1, g1, sh2, sc2, g2 = jnp.split(mod_l, 6, axis=-1)
        csh1, csc1, cg1, csh2, csc2, cg2 = jnp.split(mod_c, 6, axis=-1)
        hx = _modulate(_rmsnorm(x, g_norm1[l]), sh1, sc1)
        hc = _modulate(_rmsnorm(h, g_norm1[l]), csh1, csc1)
        y_lat, y_ctx = _mixer(hx, hc, rope, not last, w_in[l], conv_w[l], conv_b[l], lru_w_a[l], lru_b_a[l],
                              lru_w_x[l], lru_b_x[l], lru_lambda[l], g_q_lat[l], w_uq[l], g_kv_lat[l],
                              w_ukv[l], g_qn[l], g_kn[l], w_pool[l], pool_scale[l], w_out[l])
        x = x + g1 * y_lat
        x = x + g2 * _mlp(_modulate(_rmsnorm(x, g_norm2[l]), sh2, sc2), w_ff1[l], w_ff2[l])
        if not last:
            h = h + cg1 * y_ctx
            h = h + cg2 * _mlp(_modulate(_rmsnorm(h, g_norm2[l]), csh2, csc2), w_ff1[l], w_ff2[l])
    return x
```

```python
import math
import numpy as np
import ml_dtypes
from contextlib import ExitStack
import concourse.bass as bass
import concourse.mybir as mybir
from concourse.bass_utils import run_bass_kernel_spmd

F32 = mybir.dt.float32
BF16 = mybir.dt.bfloat16
AF = mybir.ActivationFunctionType
ALU = mybir.AluOpType
AX = mybir.AxisListType

D = 1024
KD = 8
DIN = 1440
DFF = 4096
NH = 8
DQK = 96
EPS = 1e-6
PAD = 8
GRID_W = 64
OVERLAP_B2 = True
DRAIN_FIRST = False
BG_STRIDE = 7
FILL = 0


class Buf:
    __slots__ = ("t", "w", "r", "sem", "gsem", "name")

    def __init__(self, t=None, name=""):
        self.t = t
        self.w = None
        self.r = []
        self.sem = None
        self.gsem = None
        self.name = name

    def __getitem__(self, k):
        return self.t[k]


class Rot:
    def __init__(self, bufs):
        self.bufs = bufs
        self.i = 0

    def next(self):
        b = self.bufs[self.i % len(self.bufs)]
        self.i += 1
        return b


class FW:
    def __init__(self, nc, es):
        self.nc = nc
        self.es = es
        self.scopes = []
        self.engs = {"pe": nc.tensor, "act": nc.scalar, "dve": nc.vector, "pool": nc.gpsimd, "sp": nc.sync}
        self.sems = {}
        self.cnt = {}
        self.waited = {k: {} for k in self.engs}
        for k in self.engs:
            self.sems[k] = es.enter_context(nc.semaphore("s_" + k))
            self.cnt[k] = 0
        self.sems["bar"] = es.enter_context(nc.semaphore("s_bar"))
        self.cnt["bar"] = 0
        self.nsem = 0
        self.free_sems = []
        self.free_gsems = []
        self.scope_bufs = []
        self.uid = 0

    def _name(self, name):
        self.uid += 1
        return "%s_%d" % (name, self.uid)

    def sbuf(self, name, shape, dt):
        b = Buf(self.es.enter_context(self.nc.sbuf_tensor(self._name(name), shape, dt)), name)
        if self.scope_bufs:
            self.scope_bufs[-1].append(b)
        return b

    def psum(self, name, shape, dt=F32):
        b = Buf(self.es.enter_context(self.nc.psum_tensor(self._name(name), shape, dt)), name)
        if self.scope_bufs:
            self.scope_bufs[-1].append(b)
        return b

    def dram(self, name, shape, dt, kind="Internal"):
        return Buf(self.nc.dram_tensor(name, shape, dt, kind=kind), name)

    def push_scope(self):
        self.scopes.append(self.es)
        self.es = ExitStack()
        self.es.__enter__()
        self.scope_bufs.append([])

    def pop_scope(self):
        self.barrier()
        for b in self.scope_bufs.pop():
            if b.sem is not None:
                self.free_sems.append(b.sem)
                b.sem = None
            if b.gsem is not None:
                self.free_gsems.append(b.gsem)
                b.gsem = None
        self.es.__exit__(None, None, None)
        self.es = self.scopes.pop()

    def _wait(self, e, s, v):
        wd = self.waited[e]
        if wd.get(s, 0) < v:
            self.engs[e].wait_ge(self.sems[s], v)
            wd[s] = v

    def _waits(self, e, reads, writes):
        deps = {}
        for b in reads:
            if b.w is not None:
                s, v = b.w
                if deps.get(s, 0) < v:
                    deps[s] = v
        for b in writes:
            if b.w is not None:
                s, v = b.w
                if deps.get(s, 0) < v:
                    deps[s] = v
            for (s, v) in b.r:
                if deps.get(s, 0) < v:
                    deps[s] = v
        for s, v in deps.items():
            if e == "pe" and s == "pe":
                continue
            self._wait(e, s, v)

    def _mark(self, tok, reads, writes):
        for b in reads:
            b.r.append(tok)
            if len(b.r) > 64:
                mx = {}
                for (s, v) in b.r:
                    if mx.get(s, 0) < v:
                        mx[s] = v
                b.r = list(mx.items())
        for b in writes:
            b.w = tok
            b.r = []

    def op(self, e, fn, reads=(), writes=()):
        self._waits(e, reads, writes)
        ins = fn(self.engs[e])
        self.cnt[e] += 1
        ins.then_inc(self.sems[e], 1)
        self._mark((e, self.cnt[e]), reads, writes)
        return ins

    def dma(self, q, out, in_, sb, reads=(), writes=(), **kw):
        self._waits(q, reads, writes)
        attr, pool_, pre = ("gsem", self.free_gsems, "g") if q == "pool" else ("sem", self.free_sems, "d")
        if getattr(sb, attr) is None:
            if pool_:
                setattr(sb, attr, pool_.pop())
            else:
                key = "%s%d" % (pre, self.nsem)
                self.nsem += 1
                self.sems[key] = self.scopes[0].enter_context(self.nc.semaphore(key)) if self.scopes else \
                    self.es.enter_context(self.nc.semaphore(key))
                self.cnt[key] = 0
                setattr(sb, attr, key)
        key = getattr(sb, attr)
        ins = self.engs[q].dma_start(out=out, in_=in_, **kw)
        self.cnt[key] += 16
        ins.then_inc(self.sems[key], 16)
        self._mark((key, self.cnt[key]), reads, writes)
        return ins

    def barrier(self):
        for s, v in self.cnt.items():
            if s in ("sp", "bar") or v == 0:
                continue
            self._wait("sp", s, v)
        self.engs["sp"].sem_inc(self.sems["bar"], 1)
        self.cnt["bar"] += 1
        for e in ("pe", "act", "dve", "pool"):
            self._wait(e, "bar", self.cnt["bar"])

    def finish(self):
        for s, v in self.cnt.items():
            if s in ("sp", "bar") or v == 0:
                continue
            self._wait("sp", s, v)


class Prog:
    def __init__(self, L, C, NL, debug=False):
        self.L, self.C, self.NL, self.debug = L, C, NL, debug
        self.T = L + C
        self.TP = self.T + 3 * PAD
        self.groups = []
        t = 0
        while t < C:
            n = min(512, C - t)
            self.groups.append((t, n, 1))
            t += n
        while t < self.T:
            n = min(512, self.T - t)
            self.groups.append((t, n, 0))
            t += n
        self.NG = len(self.groups)

    def ppos(self, t):
        return PAD + t if t < self.C else 2 * PAD + t

    def build(self):
        nc = bass.Bass("TRN2", target_bir_lowering=False)
        self.nc = nc
        L, C, T, NL = self.L, self.C, self.T, self.NL
        with ExitStack() as es:
            fw = FW(nc, es)
            self.fw = fw
            I = lambda name, shape, dt=F32: fw.dram(name, shape, dt, kind="ExternalInput")
            self.x = I("x", [L, D])
            self.ctx = I("ctx", [C, D])
            self.cvec = I("cvec", [2, D])
            self.w_mod = I("w_mod", [NL, D, 6 * D])
            self.b_mod = I("b_mod", [NL, 6 * D])
            self.g_norm1 = I("g_norm1", [NL, D])
            self.g_norm2 = I("g_norm2", [NL, D])
            self.w_in = I("w_in", [NL, D, DIN])
            self.conv_w = I("conv_w", [NL, 4, 256])
            self.conv_b = I("conv_b", [NL, 256])
            self.lru_w_a = I("lru_w_a", [NL, 2, 4, 64, 64])
            self.lru_b_a = I("lru_b_a", [NL, 2, 256])
            self.lru_w_x = I("lru_w_x", [NL, 2, 4, 64, 64])
            self.lru_b_x = I("lru_b_x", [NL, 2, 256])
            self.lru_lambda = I("lru_lambda", [NL, 2, 256])
            self.g_q_lat = I("g_q_lat", [NL, 384])
            self.w_uq = I("w_uq", [NL, 384, 768])
            self.g_kv_lat = I("g_kv_lat", [NL, 256])
            self.w_ukv = I("w_ukv", [NL, 256, 1024])
            self.g_qn = I("g_qn", [NL, 96])
            self.g_kn = I("g_kn", [NL, 96])
            self.w_pool = I("w_pool", [NL, 4, 64, 64])
            self.pool_scale = I("pool_scale", [NL, 256])
            self.w_out = I("w_out", [NL, D, D])
            self.w_ff1 = I("w_ff1", [NL, D, DFF])
            self.w_ff2 = I("w_ff2", [NL, DFF, D])
            self.ident_d = I("ident", [128, 128], BF16)
            self.ropeC = I("ropeC", [T, 32])
            self.ropeS = I("ropeS", [T, 32])
            self.pcorr = I("pcorr", [128, 2, 2, 8])
            self.pinvw = I("pinvw", [128, 2])
            self.out = fw.dram("out", [L, D], F32, kind="ExternalOutput")

            skind = "ExternalOutput" if self.debug else "Internal"
            S = lambda name, shape, dt: fw.dram(name, shape, dt, kind=skind)
            NG = self.NG
            self.modv = S("modv", [NL, 2, 6 * D], F32)
            self.zlx = S("zlx", [256, T], F32)
            self.zgl = S("zgl", [256, T], BF16)
            self.zpu = S("zpu", [256, T], F32)
            self.zq = S("zq", [672, T], BF16)
            self.qT = S("qT", [NH, DQK, T], BF16)
            self.kT = S("kT", [NH, DQK, T], BF16)
            self.vv = S("vv", [T, 512], BF16)
            self.mixd = S("mixd", [D, T], BF16)
            self.x1d = S("x1d", [T, D], F32)
            self.h2d = S("h2d", [D, T], BF16)
            self.xres = S("xres", [T, D], F32)
            P = lambda nm: [Buf(None, "%s%d" % (nm, g)) for g in range(NG)]
            self.p_modv = [Buf(None, "modv%d" % l) for l in range(NL)]
            self.p_zlx, self.p_zgl, self.p_zpu, self.p_zq = P("zlx"), P("zgl"), P("zpu"), P("zq")
            self.p_qT, self.p_kT, self.p_vv = P("qT"), P("kT"), P("vv")
            self.p_mixA, self.p_mixB, self.p_mixC = P("mixA"), P("mixB"), P("mixC")
            self.p_x1d, self.p_h2d, self.p_xres = P("x1d"), P("h2d"), P("xres")
            self.p_out = P("out")

            self.idb = fw.sbuf("idb", [128, 128], BF16)
            fw.dma("sp", self.idb[:], self.ident_d[:], self.idb, writes=[self.idb])
            self.onesb = fw.sbuf("onesb", [128, 128], BF16)
            fw.op("dve", lambda e: e.memset(self.onesb[:], 1.0), writes=[self.onesb])
            self.onesf = fw.sbuf("onesf", [128, 128], F32)
            fw.op("dve", lambda e: e.memset(self.onesf[:], 1.0), writes=[self.onesf])
            self.cst = fw.sbuf("cst", [128, 8], F32)
            for j, v in enumerate((1024 * EPS, EPS, 96 * EPS, 1.0, 0.0)):
                fw.op("dve", lambda e, j=j, v=v: e.memset(self.cst[:, j:j + 1], v), writes=[self.cst])
            self.colmod = [fw.sbuf("colmod%d" % l, [128, 48, 2], F32) for l in range(NL)]

            self.marks = []
            mark = lambda nm: self.marks.append((nm, dict(fw.cnt)))
            mark("mod")
            self.phase_mod()
            for l in range(NL):
                last = (l == NL - 1)
                mark("vec%d" % l)
                self.phase_vec(l)
                mark("a1%d" % l)
                self.phase_a1(l)
                mark("a2%d" % l)
                self.phase_a2(l)
                mark("b1%d" % l)
                self.phase_b1(l)
                mark("b2%d" % l)
                if not OVERLAP_B2:
                    self.phase_b2(l)
                mark("c%d" % l)
                self.phase_c(l, last, bg=OVERLAP_B2)
                mark("d1%d" % l)
                self.phase_d1(l, last)
                mark("d2%d" % l)
                self.phase_d2(l, last)
                self.end_vec()
            mark("end")
            fw.finish()
        return nc

    def col_load(self, q, dst_buf, dst_ap, src_ap, reads=()):
        self.fw.dma(q, dst_ap, src_ap, dst_buf, reads=list(reads), writes=[dst_buf], allow_slow_non_contiguous=True)

    def phase_mod(self):
        fw, NL = self.fw, self.NL
        fw.push_scope()
        cact = fw.sbuf("cact", [128, 8, 2], F32)
        for j in range(2):
            self.col_load("sp", cact, cact[:, :, j], self.cvec.t[j].rearrange("(k p) -> p k", p=128))
        fw.op("act", lambda e: e.activation(out=cact[:], in_=cact[:], func=AF.Silu), reads=[cact], writes=[cact])
        wrot = Rot([fw.sbuf("wm%d" % i, [128, 1536], F32) for i in range(4)])
        psm = [fw.psum("psm%d" % i, [128, 512], F32) for i in range(3)]
        pcol = fw.psum("pcol", [128, 96], F32)
        id2 = fw.sbuf("id2", [2, 2], F32)
        fw.op("dve", lambda e: e.memset(id2[:], 0.0), writes=[id2])
        fw.op("dve", lambda e: e.memset(id2[0:1, 0:1], 1.0), writes=[id2])
        fw.dma("sp", id2[1:2, 1:2], self.onesf[0:1, 0:1], id2, reads=[self.onesf], writes=[id2])
        for l in range(NL):
            bm = fw.sbuf("bm", [2, 6 * D], F32)
            fw.dma("sp", bm[:], self.b_mod.t[l].unsqueeze(0).to_broadcast([2, 6 * D]), bm, writes=[bm])
            msb = fw.sbuf("msb", [2, 6 * D], F32)
            for nq in range(4):
                for k in range(KD):
                    wt = wrot.next()
                    fw.dma("sp", wt[:], self.w_mod.t[l, k * 128:(k + 1) * 128, nq * 1536:(nq + 1) * 1536], wt, writes=[wt])
                    for j in range(3):
                        fw.op("pe", lambda e, j=j, k=k, wt=wt: e.matmul(psm[j][0:2, :], lhsT=cact[:, k, :], rhs=wt[:, j * 512:(j + 1) * 512],
                                                                       start=(k == 0), stop=(k == KD - 1)),
                              reads=[cact, wt], writes=[psm[j]])
                for j in range(3):
                    c0 = nq * 1536 + j * 512
                    fw.op("dve", lambda e, j=j, c0=c0: e.tensor_tensor(out=msb[:, c0:c0 + 512], in0=psm[j][0:2, :], in1=bm[:, c0:c0 + 512], op=ALU.add),
                          reads=[psm[j], bm], writes=[msb])
            fw.dma("sp", self.modv.t[l], msb[:], msb, reads=[msb], writes=[self.p_modv[l]])
            for ck in range(48):
                fw.op("pe", lambda e, ck=ck: e.matmul(pcol[:, 2 * ck:2 * ck + 2], lhsT=msb[0:2, ck * 128:(ck + 1) * 128], rhs=id2[:, :],
                                                      start=True, stop=True), reads=[msb, id2], writes=[pcol])
            fw.op("dve", lambda e, l=l: e.tensor_copy(out=self.colmod[l][:].rearrange("p a b -> p (a b)"), in_=pcol[:, :]),
                  reads=[pcol], writes=[self.colmod[l]])
        fw.pop_scope()

    def phase_vec(self, l):
        fw = self.fw
        fw.push_scope()
        V = {}
        self.V = V
        cm = self.colmod[l]

        def colvec(name, src_ap, k):
            b = fw.sbuf(name, [128, k], F32)
            self.col_load("sp", b, b[:], src_ap.rearrange("(k p) -> p k", p=128))
            return b
        gn1 = colvec("gn1", self.g_norm1.t[l], 8)
        gn2 = colvec("gn2", self.g_norm2.t[l], 8)
        for nm, gn, i_sh, i_sc in (("1", gn1, 0, 1), ("2", gn2, 3, 4)):
            for seg in range(2):
                G = fw.sbuf("G%s_%d" % (nm, seg), [128, 8], F32)
                SH = fw.sbuf("SH%s_%d" % (nm, seg), [128, 8], F32)
                fw.op("dve", lambda e, G=G, i_sc=i_sc, seg=seg: e.tensor_scalar(out=G[:], in0=cm[:, i_sc * 8:(i_sc + 1) * 8, seg], scalar1=1.0, scalar2=32.0,
                                                                              op0=ALU.add, op1=ALU.mult), reads=[cm], writes=[G])
                fw.op("dve", lambda e, G=G, gn=gn: e.tensor_tensor(out=G[:], in0=G[:], in1=gn[:], op=ALU.mult), reads=[G, gn], writes=[G])
                fw.op("dve", lambda e, SH=SH, i_sh=i_sh, seg=seg: e.tensor_copy(out=SH[:], in_=cm[:, i_sh * 8:(i_sh + 1) * 8, seg]), reads=[cm], writes=[SH])
                V["G" + nm, seg] = G
                V["SH" + nm, seg] = SH
        cw = fw.sbuf("cw", [128, 4, 2], F32)
        for k in range(4):
            self.col_load("sp", cw, cw[:, k, :], self.conv_w.t[l, k].rearrange("(c p) -> p c", p=128))
        V["cw"] = cw
        V["cb"] = colvec("cb", self.conv_b.t[l], 2)
        for nm, src in (("ba", self.lru_b_a), ("bx", self.lru_b_x), ("lam", self.lru_lambda)):
            b = fw.sbuf(nm, [128, 2, 2], F32)
            for d in range(2):
                self.col_load("sp", b, b[:, d, :], src.t[l, d].rearrange("(c p) -> p c", p=128))
            V[nm] = b
        for nm in ("ba", "bx"):
            nb = fw.sbuf("n" + nm, [128, 2, 2], F32)
            fw.op("dve", lambda e, nb=nb, nm=nm: e.tensor_scalar(out=nb[:], in0=V[nm][:], scalar1=-1.0, scalar2=None, op0=ALU.mult), reads=[V[nm]], writes=[nb])
            V["n" + nm] = nb
        cA = fw.sbuf("cA", [128, 2, 2], F32)
        fw.op("act", lambda e: e.activation(out=cA[:], in_=V["lam"][:], func=AF.Exp, scale=-1.0), reads=[V["lam"]], writes=[cA])
        fw.op("act", lambda e: e.activation(out=cA[:], in_=cA[:], func=AF.Ln, bias=self.cst[:, 3:4], scale=1.0), reads=[cA, self.cst], writes=[cA])
        fw.op("dve", lambda e: e.tensor_scalar(out=cA[:], in0=cA[:], scalar1=-8.0, scalar2=None, op0=ALU.mult), reads=[cA], writes=[cA])
        V["cA"] = cA
        V["psc"] = colvec("psc", self.pool_scale.t[l], 2)
        V["gq"] = colvec("gq", self.g_q_lat.t[l], 3)
        V["gkv"] = colvec("gkv", self.g_kv_lat.t[l], 2)
        GQ = fw.sbuf("GQ", [128, 96], F32)
        GK = fw.sbuf("GK", [128, 96], F32)
        fw.dma("sp", GQ[:], self.g_qn.t[l].unsqueeze(0).to_broadcast([128, 96]), GQ, writes=[GQ])
        fw.dma("sp", GK[:], self.g_kn.t[l].unsqueeze(0).to_broadcast([128, 96]), GK, writes=[GK])
        fw.op("dve", lambda e: e.tensor_scalar(out=GK[:], in0=GK[:], scalar1=math.sqrt(96.0), scalar2=None, op0=ALU.mult), reads=[GK], writes=[GK])
        V["GQ"], V["GK"] = GQ, GK

    def end_vec(self):
        self.fw.pop_scope()

    def res_src(self, l, t0, n):
        if l == 0:
            if t0 < self.C:
                return self.ctx.t[t0:t0 + n, :]
            return self.x.t[t0 - self.C:t0 - self.C + n, :]
        return self.xres.t[t0:t0 + n, :]

    def norm_part(self, xts, nsub, junk, ss, rp, xn):
        fw = self.fw
        fw.op("dve", lambda e: e.memset(ss[:], 0.0), writes=[ss])
        for s in range(nsub):
            fw.op("act", lambda e, s=s: e.activation(out=junk[:], in_=xts[s][:], func=AF.Square, accum_out=ss[:, s:s + 1]),
                  reads=[xts[s]], writes=[junk, ss])
        fw.op("act", lambda e: e.activation(out=rp[:, 0:nsub], in_=ss[:, 0:nsub], func=AF.Sqrt, bias=self.cst[:, 0:1], scale=1.0),
              reads=[ss, self.cst], writes=[rp])
        fw.op("dve", lambda e: e.reciprocal(out=rp[:, 0:nsub], in_=rp[:, 0:nsub]), reads=[rp], writes=[rp])
        for s in range(nsub):
            fw.op("dve", lambda e, s=s: e.tensor_scalar(out=xn[s][:], in0=xts[s][:], scalar1=rp[:, s:s + 1], scalar2=None, op0=ALU.mult),
                  reads=[xts[s], rp], writes=[xn[s]])

    def transpose_part(self, xn, nsub, G, SH, hT, ptr_rot):
        fw = self.fw
        n = nsub * 128
        for cp in range(4):
            ptr = ptr_rot.next()
            for c2 in range(2):
                k = 2 * cp + c2
                for s in range(nsub):
                    fw.op("pe", lambda e, k=k, s=s, c2=c2, ptr=ptr: e.transpose(out=ptr[:, c2 * 512 + s * 128:c2 * 512 + (s + 1) * 128],
                                                                                in_=xn[s][:, k * 128:(k + 1) * 128], identity=self.idb[:]),
                          reads=[xn[s], self.idb], writes=[ptr])
            for c2 in range(2):
                k = 2 * cp + c2
                if c2 == 0:
                    fw.op("dve", lambda e, k=k, c2=c2, ptr=ptr: e.tensor_scalar(out=hT[:, k, 0:n], in0=ptr[:, c2 * 512:c2 * 512 + n], scalar1=G[:, k:k + 1],
                                                                                scalar2=SH[:, k:k + 1], op0=ALU.mult, op1=ALU.add),
                          reads=[ptr, G, SH], writes=[hT])
                else:
                    fw.op("act", lambda e, k=k, c2=c2, ptr=ptr: e.activation(out=hT[:, k, 0:n], in_=ptr[:, c2 * 512:c2 * 512 + n], func=AF.Identity,
                                                                             scale=G[:, k:k + 1], bias=SH[:, k:k + 1]),
                          reads=[ptr, G, SH], writes=[hT])

    def phase_a1(self, l):
        fw, V = self.fw, self.V
        fw.push_scope()
        win = fw.sbuf("win", [128, KD, DIN], BF16)
        for k in range(KD):
            fw.dma("pool", win[:, k, :], self.w_in.t[l, k * 128:(k + 1) * 128, :], win, writes=[win])
        xsets = [[fw.sbuf("xt%d_%d" % (i, s), [128, D], F32) for s in range(4)] for i in range(2)]
        xnS = [[fw.sbuf("xn%d_%d" % (i, s), [128, D], BF16) for s in range(4)] for i in range(2)]
        junk = fw.sbuf("junk", [128, D], BF16)
        ssS = [fw.sbuf("ss%d" % i, [128, 4], F32) for i in range(2)]
        rpS = [fw.sbuf("rp%d" % i, [128, 4], F32) for i in range(2)]
        hTS = [fw.sbuf("hT%d" % i, [128, KD, 512], BF16) for i in range(2)]
        ptr_rot = Rot([fw.psum("ptr%d" % i, [128, 1024], BF16) for i in range(2)])
        pz_rot = Rot([fw.psum("pz%d" % i, [128, 512], F32) for i in range(4)])
        stf = Rot([fw.sbuf("stf%d" % i, [128, 512], F32) for i in range(4)])
        stb = Rot([fw.sbuf("stb%d" % i, [128, 512], BF16) for i in range(8)])
        chunks = [(0, 128, "lx", 0), (128, 128, "lx", 128), (256, 128, "lg", 0), (384, 128, "lg", 128),
                  (512, 128, "q", 0), (640, 128, "q", 128), (768, 128, "q", 256), (896, 128, "q", 384), (1024, 128, "q", 512),
                  (1152, 32, "q", 640), (1184, 128, "pu", 0), (1312, 128, "pu", 128)]

        def load(gi):
            t0, n, seg = self.groups[gi]
            xs = xsets[gi % 2]
            rd = [self.p_xres[gi]] if l > 0 else []
            for s in range(n // 128):
                fw.dma("sp", xs[s][:], self.res_src(l, t0 + s * 128, 128), xs[s], reads=rd, writes=[xs[s]])

        NG = self.NG

        def stN(gi):
            t0, n, seg = self.groups[gi]
            self.norm_part(xsets[gi % 2], n // 128, junk, ssS[gi % 2], rpS[gi % 2], xnS[gi % 2])

        def stT(gi):
            t0, n, seg = self.groups[gi]
            self.transpose_part(xnS[gi % 2], n // 128, V["G1", seg], V["SH1", seg], hTS[gi % 2], ptr_rot)

        load(0)
        if NG > 1:
            load(1)
        stN(0)
        if NG > 2:
            load(2)
        if NG > 1:
            stN(1)
        stT(0)
        for gi, (t0, n, seg) in enumerate(self.groups):
            if gi + 2 < NG:
                stN(gi + 2)
            if gi + 3 < NG:
                load(gi + 3)
            if gi + 1 < NG:
                stT(gi + 1)
            nsub = n // 128
            hT = hTS[gi % 2]
            for ci, (c0, m, kind, r0) in enumerate(chunks):
                pz = pz_rot.next()
                for k in range(KD):
                    fw.op("pe", lambda e, k=k, c0=c0, m=m, pz=pz: e.matmul(pz[0:m, 0:n], lhsT=win[:, k, c0:c0 + m], rhs=hT[:, k, 0:n],
                                                                           start=(k == 0), stop=(k == KD - 1)), reads=[win, hT], writes=[pz])
                if kind == "lx":
                    st = stf.next()
                    fw.op("dve", lambda e, st=st, pz=pz: e.tensor_copy(out=st[:, 0:n], in_=pz[:, 0:n]), reads=[pz], writes=[st])
                    fw.dma("sp", self.zlx.t[r0:r0 + 128, t0:t0 + n], st[:, 0:n], st, reads=[st], writes=[self.p_zlx[gi]])
                elif kind == "pu":
                    st = stf.next()
                    fw.op("act", lambda e, st=st, pz=pz: e.activation(out=st[:, 0:n], in_=pz[:, 0:n], func=AF.Copy), reads=[pz], writes=[st])
                    fw.dma("sp", self.zpu.t[r0:r0 + 128, t0:t0 + n], st[:, 0:n], st, reads=[st], writes=[self.p_zpu[gi]])
                elif kind == "lg":
                    st = stb.next()
                    fw.op("act", lambda e, st=st, pz=pz: e.activation(out=st[:, 0:n], in_=pz[:, 0:n], func=AF.Gelu_apprx_tanh), reads=[pz], writes=[st])
                    fw.dma("sp", self.zgl.t[r0:r0 + 128, t0:t0 + n], st[:, 0:n], st, reads=[st], writes=[self.p_zgl[gi]])
                else:
                    st = stb.next()
                    eng = "dve" if ci % 2 == 0 else "act"
                    if eng == "dve":
                        fw.op("dve", lambda e, st=st, pz=pz, m=m: e.tensor_copy(out=st[0:m, 0:n], in_=pz[0:m, 0:n]), reads=[pz], writes=[st])
                    else:
                        fw.op("act", lambda e, st=st, pz=pz, m=m: e.activation(out=st[0:m, 0:n], in_=pz[0:m, 0:n], func=AF.Copy), reads=[pz], writes=[st])
                    fw.dma("sp", self.zq.t[r0:r0 + m, t0:t0 + n], st[0:m, 0:n], st, reads=[st], writes=[self.p_zq[gi]])
        fw.pop_scope()

    def phase_a2(self, l):
        fw, V = self.fw, self.V
        fw.push_scope()
        wuq = fw.sbuf("wuq", [128, 3, 768], BF16)
        wukv = fw.sbuf("wukv", [128, 2, 1024], BF16)
        wst = fw.sbuf("wst", [128, 1024], F32)
        for k in range(3):
            fw.dma("sp", wst[:, 0:768], self.w_uq.t[l, k * 128:(k + 1) * 128, :], wst, writes=[wst])
            fw.op("dve", lambda e, k=k: e.tensor_scalar(out=wuq[:, k, :], in0=wst[:, 0:768], scalar1=V["gq"][:, k:k + 1], scalar2=None, op0=ALU.mult),
                  reads=[wst, V["gq"]], writes=[wuq])
        for k in range(2):
            fw.dma("sp", wst[:], self.w_ukv.t[l, k * 128:(k + 1) * 128, :], wst, writes=[wst])
            fw.op("dve", lambda e, k=k: e.tensor_scalar(out=wukv[:, k, :], in0=wst[:], scalar1=V["gkv"][:, k:k + 1], scalar2=None, op0=ALU.mult),
                  reads=[wst, V["gkv"]], writes=[wukv])
        S96 = math.sqrt(96.0)
        gcol = fw.sbuf("gcol", [128, 2], F32)
        fw.op("dve", lambda e: e.memset(gcol[:], 1.0), writes=[gcol])
        self.col_load("sp", gcol, gcol[0:64, 0:1], self.g_qn.t[l, 0:64].unsqueeze(1))
        self.col_load("sp", gcol, gcol[0:64, 1:2], self.g_kn.t[l, 0:64].unsqueeze(1))
        fw.op("dve", lambda e: e.tensor_scalar(out=gcol[0:64, 1:2], in0=gcol[0:64, 1:2], scalar1=S96, scalar2=None, op0=ALU.mult), reads=[gcol], writes=[gcol])
        gr = {}
        for nm, src, mul in (("q", self.g_qn, 1.0), ("k", self.g_kn, S96)):
            g_c = fw.sbuf("grc" + nm, [128, 32], F32)
            g_s = fw.sbuf("grs" + nm, [128, 32], F32)
            fw.dma("sp", g_c[:], src.t[l, 64:96].unsqueeze(0).to_broadcast([128, 32]), g_c, writes=[g_c])
            for a_ in range(2):
                for b_ in range(2):
                    o = a_ * 16 + b_ * 8
                    o2 = 64 + a_ * 16 + (1 - b_) * 8
                    fw.dma("sp", g_s[:, o:o + 8], src.t[l, o2:o2 + 8].unsqueeze(0).to_broadcast([128, 8]), g_s, writes=[g_s])
            if mul != 1.0:
                fw.op("dve", lambda e, g_c=g_c: e.tensor_scalar(out=g_c[:], in0=g_c[:], scalar1=mul, scalar2=None, op0=ALU.mult), reads=[g_c], writes=[g_c])
                fw.op("dve", lambda e, g_s=g_s: e.tensor_scalar(out=g_s[:], in0=g_s[:], scalar1=mul, scalar2=None, op0=ALU.mult), reads=[g_s], writes=[g_s])
            gr[nm] = (g_c, g_s)
        zsets = [(fw.sbuf("zqa%d" % i, [128, 5, 512], BF16), fw.sbuf("zkr%d" % i, [32, 512], BF16),
                  fw.sbuf("rc%d" % i, [128, 4, 32], F32), fw.sbuf("rs%d" % i, [128, 4, 32], F32),
                  fw.sbuf("rcq%d" % i, [128, 4, 32], F32), fw.sbuf("rsq%d" % i, [128, 4, 32], F32),
                  fw.sbuf("rck%d" % i, [128, 4, 32], F32), fw.sbuf("rsk%d" % i, [128, 4, 32], F32)) for i in range(2)]
        sq = fw.sbuf("sq", [128, 5, 512], BF16)
        pss = fw.psum("pss", [128, 2], F32)
        pq = fw.psum("pq", [128, 1024], F32)
        pkv = fw.psum("pkv", [128, 1024], F32)
        pkr = fw.psum("pkr", [128, 1024], BF16)
        ptq = fw.psum("ptq", [128, 1024], BF16)
        ptk = fw.psum("ptk", [128, 1024], BF16)
        NB_ = 2
        mk = lambda nm, shape, dt: [fw.sbuf("%s%d" % (nm, i), shape, dt) for i in range(NB_)]
        r2 = mk("r2", [128, 2], F32)
        qs = mk("qs", [128, 8, 96], F32)
        ks = mk("ks", [128, 8, 96], F32)
        tmpq = mk("tmpq", [128, 8, 96], F32)
        tmpk = mk("tmpk", [128, 8, 96], F32)
        ssq = mk("ssq", [128, 8], F32)
        ssk = mk("ssk", [128, 8], F32)
        xrq = mk("xrq", [128, 8, 32], F32)
        xrk = mk("xrk", [128, 8, 32], F32)
        t1q = mk("t1q", [128, 8, 32], F32)
        t2q = mk("t2q", [128, 8, 32], F32)
        t1k = mk("t1k", [128, 8, 32], F32)
        t2k = mk("t2k", [128, 8, 32], F32)
        qb = mk("qb", [128, 8, 96], BF16)
        kb = mk("kb", [128, 8, 96], BF16)
        osets = [(fw.sbuf("qTst%d" % i, [96, 8, 512], BF16), fw.sbuf("kTst%d" % i, [96, 8, 512], BF16),
                  fw.sbuf("vst%d" % i, [128, 4, 512], BF16)) for i in range(2)]

        def load(gi):
            t0, n, seg = self.groups[gi]
            za, zk, rc, rs = zsets[gi % 2][0:4]
            fw.dma("sp", za[:, :, 0:n], self.zq.t[0:640, t0:t0 + n].rearrange("(k p) t -> p k t", p=128), za, reads=[self.p_zq[gi]], writes=[za])
            fw.dma("sp", zk[:, 0:n], self.zq.t[640:672, t0:t0 + n], zk, reads=[self.p_zq[gi]], writes=[zk])
            nsub = n // 128
            fw.dma("sp", rc[:, 0:nsub, :], self.ropeC.t[t0:t0 + n, :].rearrange("(s p) d -> p s d", p=128), rc, writes=[rc])
            fw.dma("sp", rs[:, 0:nsub, :], self.ropeS.t[t0:t0 + n, :].rearrange("(s p) d -> p s d", p=128), rs, writes=[rs])

        def group_pro(gi):
            t0, n, seg = self.groups[gi]
            nsub = n // 128
            za, zk, rc, rs, rcq, rsq, rck, rsk = zsets[gi % 2]
            fw.op("pool", lambda e: e.tensor_tensor(out=sq[:, :, 0:n], in0=za[:, :, 0:n], in1=za[:, :, 0:n], op=ALU.mult), reads=[za], writes=[sq])
            for dst, srcb, g_ in ((rcq, rc, gr["q"][0]), (rsq, rs, gr["q"][1]), (rck, rc, gr["k"][0]), (rsk, rs, gr["k"][1])):
                fw.op("pool", lambda e, dst=dst, srcb=srcb, g_=g_: e.tensor_tensor(out=dst[:, 0:nsub, :], in0=srcb[:, 0:nsub, :],
                                                                                 in1=g_[:].unsqueeze(1).to_broadcast([128, nsub, 32]), op=ALU.mult),
                      reads=[srcb, g_], writes=[dst])

        def stageA(idx):
            gi, s = items[idx]
            par = idx % NB_
            t0, n, seg = self.groups[gi]
            za, zk = zsets[gi % 2][0:2]
            vst = osets[gi % 2][2]
            sl = slice(s * 128, (s + 1) * 128)
            for k in range(3):
                fw.op("pe", lambda e, k=k: e.matmul(pss[:, 0:1], lhsT=sq[:, k, sl], rhs=self.onesb[:, 0:1], start=(k == 0), stop=(k == 2)),
                      reads=[sq, self.onesb], writes=[pss])
            for k in range(2):
                fw.op("pe", lambda e, k=k: e.matmul(pss[:, 1:2], lhsT=sq[:, 3 + k, sl], rhs=self.onesb[:, 0:1], start=(k == 0), stop=(k == 1)),
                      reads=[sq, self.onesb], writes=[pss])
            r2_ = r2[par]
            fw.op("act", lambda e: e.activation(out=r2_[:, 0:1], in_=pss[:, 0:1], func=AF.Sqrt, bias=self.cst[:, 1:2], scale=1.0 / 384.0),
                  reads=[pss, self.cst], writes=[r2_])
            fw.op("act", lambda e: e.activation(out=r2_[:, 1:2], in_=pss[:, 1:2], func=AF.Sqrt, bias=self.cst[:, 1:2], scale=1.0 / 256.0),
                  reads=[pss, self.cst], writes=[r2_])
            fw.op("dve", lambda e: e.reciprocal(out=r2_[:], in_=r2_[:]), reads=[r2_], writes=[r2_])
            for k in range(3):
                fw.op("pe", lambda e, k=k: e.matmul(pq[:, 0:512], lhsT=za[:, k, sl], rhs=wuq[:, k, 0:512], start=(k == 0), stop=(k == 2)),
                      reads=[za, wuq], writes=[pq])
            for k in range(3):
                fw.op("pe", lambda e, k=k: e.matmul(pq[:, 512:768], lhsT=za[:, k, sl], rhs=wuq[:, k, 512:768], start=(k == 0), stop=(k == 2)),
                      reads=[za, wuq], writes=[pq])
            for hf in range(2):
                for k in range(2):
                    fw.op("pe", lambda e, k=k, hf=hf: e.matmul(pkv[:, hf * 512:(hf + 1) * 512], lhsT=za[:, 3 + k, sl], rhs=wukv[:, k, hf * 512:(hf + 1) * 512],
                                                               start=(k == 0), stop=(k == 1)), reads=[za, wukv], writes=[pkv])
            fw.op("pe", lambda e: e.transpose(out=pkr[:, 0:32], in_=zk[0:32, sl], identity=self.idb[0:32, 0:32]), reads=[zk, self.idb], writes=[pkr])
            qs_, ks_ = qs[par], ks[par]
            fw.op("act", lambda e: e.activation(out=qs_[:].rearrange("p h d -> p (h d)"), in_=pq[:, 0:768], func=AF.Identity, scale=r2_[:, 0:1], bias=self.cst[:, 4:5]),
                  reads=[pq, r2_, self.cst], writes=[qs_])
            pkv3 = pkv[:, :].rearrange("p (h d) -> p h d", h=8)
            fw.op("act", lambda e: e.activation(out=ks_[:, :, 0:64], in_=pkv3[:, :, 0:64], func=AF.Identity, scale=r2_[:, 1:2], bias=self.cst[:, 4:5]),
                  reads=[pkv, r2_, self.cst], writes=[ks_])
            fw.op("dve", lambda e: e.tensor_copy(out=ks_[:, :, 64:96], in_=pkr[:, 0:32].unsqueeze(1).to_broadcast([128, 8, 32])), reads=[pkr], writes=[ks_])
            fw.op("act", lambda e: e.activation(out=vst[:, s, :].rearrange("p (h d) -> p h d", h=8), in_=pkv3[:, :, 64:128], func=AF.Identity,
                                                scale=r2_[:, 1:2], bias=self.cst[:, 4:5]), reads=[pkv, r2_, self.cst], writes=[vst])

        def chain_steps(src, tmp, ssx, xr, t1, t2, ob, rc_, rs_, s):
            xv = xr[:].rearrange("p h (a b f) -> p h a b f", a=2, b=2)
            t2v = t2[:].rearrange("p h (a b f) -> p h a b f", a=2, b=2)
            sv = rs_[:, s, :].rearrange("p (a b f) -> p a b f", a=2, b=2)
            return [
                lambda: fw.op("act", lambda e: e.activation(out=tmp[:], in_=src[:], func=AF.Square), reads=[src], writes=[tmp]),
                lambda: fw.op("dve", lambda e: e.tensor_reduce(out=ssx[:], in_=tmp[:], axis=AX.X, op=ALU.add), reads=[tmp], writes=[ssx]),
                lambda: fw.op("act", lambda e: e.activation(out=ssx[:], in_=ssx[:], func=AF.Sqrt, bias=self.cst[:, 2:3], scale=1.0), reads=[ssx, self.cst], writes=[ssx]),
                lambda: fw.op("dve", lambda e: e.reciprocal(out=ssx[:], in_=ssx[:]), reads=[ssx], writes=[ssx]),
                lambda: fw.op("dve", lambda e: e.tensor_tensor(out=ob[:, :, 0:64], in0=src[:, :, 0:64], in1=ssx[:].unsqueeze(2).to_broadcast([128, 8, 64]), op=ALU.mult),
                              reads=[src, ssx], writes=[ob]),
                lambda: fw.op("pool", lambda e: e.tensor_tensor(out=xr[:], in0=src[:, :, 64:96], in1=ssx[:].unsqueeze(2).to_broadcast([128, 8, 32]), op=ALU.mult),
                              reads=[src, ssx], writes=[xr]),
                lambda: fw.op("pool", lambda e: e.tensor_tensor(out=t1[:], in0=xr[:], in1=rc_[:, s, :].unsqueeze(1).to_broadcast([128, 8, 32]), op=ALU.mult),
                              reads=[xr, rc_], writes=[t1]),
                lambda: fw.op("pool", lambda e: e.tensor_tensor(out=t2v[:, :, :, 0, :], in0=xv[:, :, :, 1, :],
                                                                in1=sv[:, :, 0, :].unsqueeze(1).to_broadcast([128, 8, 2, 8]), op=ALU.mult), reads=[xr, rs_], writes=[t2]),
                lambda: fw.op("pool", lambda e: e.tensor_tensor(out=t2v[:, :, :, 1, :], in0=xv[:, :, :, 0, :],
                                                                in1=sv[:, :, 1, :].unsqueeze(1).to_broadcast([128, 8, 2, 8]), op=ALU.mult), reads=[xr, rs_], writes=[t2]),
                lambda: fw.op("pool", lambda e: e.tensor_tensor(out=ob[:, :, 64:96], in0=t1[:], in1=t2[:], op=ALU.add), reads=[t1, t2], writes=[ob]),
            ]

        def stageB(idx):
            gi, s = items[idx]
            par = idx % NB_
            rcq, rsq, rck, rsk = zsets[gi % 2][4:8]
            cq = chain_steps(qs[par], tmpq[par], ssq[par], xrq[par], t1q[par], t2q[par], qb[par], rcq, rsq, s)
            ck = chain_steps(ks[par], tmpk[par], ssk[par], xrk[par], t1k[par], t2k[par], kb[par], rck, rsk, s)
            for fq, fk in zip(cq, ck):
                fq()
                fk()

        def stageC(idx):
            gi, s = items[idx]
            par = idx % NB_
            t0, n, seg = self.groups[gi]
            qTst, kTst, vst = osets[gi % 2]
            sl = slice(s * 128, (s + 1) * 128)
            qb_, kb_ = qb[par], kb[par]
            for h in range(NH):
                fw.op("pe", lambda e, h=h: e.transpose(out=ptq[0:96, h * 128:(h + 1) * 128], in_=qb_[:, h, :], identity=self.idb[:]),
                      reads=[qb_, self.idb], writes=[ptq])
            for h in range(NH):
                fw.op("pe", lambda e, h=h: e.transpose(out=ptk[0:96, h * 128:(h + 1) * 128], in_=kb_[:, h, :], identity=self.idb[:]),
                      reads=[kb_, self.idb], writes=[ptk])
            fw.op("dve", lambda e: e.tensor_scalar(out=qTst[:, :, sl], in0=ptq[0:96, :].rearrange("p (h t) -> p h t", h=8), scalar1=gcol[0:96, 0:1], scalar2=None, op0=ALU.mult),
                  reads=[ptq, gcol], writes=[qTst])
            fw.op("act", lambda e: e.activation(out=kTst[:, :, sl], in_=ptk[0:96, :].rearrange("p (h t) -> p h t", h=8), func=AF.Identity,
                                                scale=gcol[0:96, 1:2], bias=self.cst[0:96, 4:5]), reads=[ptk, gcol, self.cst], writes=[kTst])
            if s == n // 128 - 1:
                nsub = n // 128
                fw.dma("sp", self.qT.t[:, :, t0:t0 + n].rearrange("h p t -> p h t"), qTst[:, :, 0:n], qTst, reads=[qTst], writes=[self.p_qT[gi]])
                fw.dma("sp", self.kT.t[:, :, t0:t0 + n].rearrange("h p t -> p h t"), kTst[:, :, 0:n], kTst, reads=[kTst], writes=[self.p_kT[gi]])
                fw.dma("sp", self.vv.t[t0:t0 + n, :].rearrange("(s p) d -> p s d", p=128), vst[:, 0:nsub, :], vst, reads=[vst], writes=[self.p_vv[gi]])

        items = [(gi, s) for gi, (t0, n, seg) in enumerate(self.groups) for s in range(n // 128)]
        NI = len(items)
        load(0)
        if self.NG > 1:
            load(1)
        group_pro(0)
        stageA(0)
        for idx in range(NI):
            gi, s = items[idx]
            if s == 0 and gi >= 1 and gi + 1 < self.NG:
                load(gi + 1)
            if idx + 1 < NI:
                if items[idx + 1][1] == 0:
                    group_pro(items[idx + 1][0])
                stageA(idx + 1)
            stageB(idx)
            if idx >= 1:
                stageC(idx - 1)
        stageC(NI - 1)
        fw.pop_scope()

    def seg_ranges(self):
        C, T = self.C, self.T
        return [((0, C), (PAD, PAD + C)), ((C, T), (2 * PAD + C, 2 * PAD + T))]

    def phase_b1(self, l):
        fw, V = self.fw, self.V
        T, TP = self.T, self.TP
        fw.push_scope()
        PU = fw.sbuf("PU", [128, 2, TP], F32)
        S2 = fw.sbuf("S2", [128, 2, TP], F32)
        S4 = fw.sbuf("S4", [128, 2, TP], F32)
        S8 = fw.sbuf("S8", [128, TP], F32)
        W = fw.sbuf("Wsel", [128, 2, TP], F32)
        M = fw.sbuf("Mx", [128, 2, T], BF16)
        WP = fw.sbuf("WP", [128, 2, 128], BF16)
        pc = fw.sbuf("pc", [128, 2, 2, 8], F32)
        piw = fw.sbuf("piw", [128, 2], F32)
        fw.dma("sp", pc[:], self.pcorr[:], pc, writes=[pc])
        fw.dma("sp", piw[:], self.pinvw[:], piw, writes=[piw])
        fw.op("pool", lambda e: e.memset(WP[:], 0.0), writes=[WP])
        for g in range(4):
            o = (g % 2) * 64
            fw.dma("pool", WP[o:o + 64, g // 2, o:o + 64], self.w_pool.t[l, g], WP, writes=[WP])
        fw.op("pool", lambda e: e.memset(PU[:], 0.0), writes=[PU])
        fw.op("dve", lambda e: e.memset(S2[:], 0.0), writes=[S2])
        fw.op("dve", lambda e: e.memset(S4[:], 0.0), writes=[S4])
        fw.op("dve", lambda e: e.memset(S8[:], 0.0), writes=[S8])
        fw.op("pool", lambda e: e.memset(W[:], 0.0), writes=[W])
        for (ta, tb), (pa, pb) in self.seg_ranges():
            fw.dma("sp", PU[:, :, pa:pb], self.zpu.t[:, ta:tb].rearrange("(c p) t -> p c t", p=128), PU, reads=self.p_zpu, writes=[PU])
        N = TP
        TT = lambda e, o, a, b: e.tensor_tensor(out=o, in0=a, in1=b, op=ALU.add)
        fw.op("pool", lambda e: TT(e, S2[:, :, 1:N], PU[:, :, 0:N - 1], PU[:, :, 1:N]), reads=[PU], writes=[S2])
        fw.op("dve", lambda e: TT(e, S4[:, :, 2:N - 1], S2[:, :, 1:N - 2], S2[:, :, 3:N]), reads=[S2], writes=[S4])
        fw.op("pool", lambda e: TT(e, S8[:, 4:N - 3], S4[:, 1, 2:N - 5], S4[:, 1, 6:N - 1]), reads=[S4], writes=[S8])
        fw.op("act", lambda e: e.activation(out=W[0:64, 0, :], in_=S2[0:64, 0, :], func=AF.Copy), reads=[S2], writes=[W])
        fw.op("act", lambda e: e.activation(out=W[64:128, 0, :], in_=S4[64:128, 0, :], func=AF.Copy), reads=[S4], writes=[W])
        fw.op("act", lambda e: e.activation(out=W[0:64, 1, :], in_=S8[0:64, :], func=AF.Copy), reads=[S8], writes=[W])
        fw.op("dve", lambda e: TT(e, W[64:128, 1, 8:N - 7], S8[64:128, 4:N - 11], S8[64:128, 12:N - 3]), reads=[S8], writes=[W])
        for (ta, tb), (pa, pb) in self.seg_ranges():
            fw.op("dve", lambda e, pa=pa: e.tensor_tensor(out=W[:, :, pa:pa + 8], in0=W[:, :, pa:pa + 8], in1=pc[:, :, 0, :], op=ALU.mult), reads=[W, pc], writes=[W])
            fw.op("dve", lambda e, pb=pb: e.tensor_tensor(out=W[:, :, pb - 8:pb], in0=W[:, :, pb - 8:pb], in1=pc[:, :, 1, :], op=ALU.mult), reads=[W, pc], writes=[W])
        for (ta, tb), (pa, pb) in self.seg_ranges():
            for c in range(2):
                fw.op("dve", lambda e, c=c, ta=ta, tb=tb, pa=pa, pb=pb: e.scalar_tensor_tensor(out=M[:, c, ta:tb], in0=W[:, c, pa:pb], scalar=piw[:, c:c + 1],
                                                                                            in1=PU[:, c, pa:pb], op0=ALU.mult, op1=ALU.subtract),
                      reads=[W, PU, piw], writes=[M])
        pp_rot = Rot([fw.psum("pp%d" % i, [128, 512], F32) for i in range(2)])
        st_rot = Rot([fw.sbuf("pst%d" % i, [128, 512], BF16) for i in range(3)])
        for gi, (t0, n, seg) in enumerate(self.groups):
            for c in range(2):
                pp = pp_rot.next()
                st = st_rot.next()
                fw.op("pe", lambda e, c=c, pp=pp: e.matmul(pp[:, 0:n], lhsT=WP[:, c, :], rhs=M[:, c, t0:t0 + n], start=True, stop=True), reads=[WP, M], writes=[pp])
                fw.op("act", lambda e, c=c, pp=pp, st=st: e.activation(out=st[:, 0:n], in_=pp[:, 0:n], func=AF.Identity, scale=V["psc"][:, c:c + 1], bias=self.cst[:, 4:5]),
                      reads=[pp, V["psc"], self.cst], writes=[st])
                fw.dma("sp", self.mixd.t[768 + c * 128:768 + (c + 1) * 128, t0:t0 + n], st[:, 0:n], st, reads=[st], writes=[self.p_mixC[gi]])
        fw.pop_scope()

    def phase_b2(self, l):
        fw, V = self.fw, self.V
        T, TP, C = self.T, self.TP, self.C
        fw.push_scope()
        WA = fw.sbuf("WA", [128, 2, 2, 128], BF16)
        WX = fw.sbuf("WX", [128, 2, 2, 128], BF16)
        fw.op("pool", lambda e: e.memset(WA[:], 0.0), writes=[WA])
        fw.op("pool", lambda e: e.memset(WX[:], 0.0), writes=[WX])
        for d in range(2):
            for h in range(4):
                o = (h % 2) * 64
                fw.dma("pool", WA[o:o + 64, d, h // 2, o:o + 64], self.lru_w_a.t[l, d, h], WA, writes=[WA])
                fw.dma("pool", WX[o:o + 64, d, h // 2, o:o + 64], self.lru_w_x.t[l, d, h], WX, writes=[WX])
        LX = fw.sbuf("LXp", [128, TP], F32)
        xc = fw.sbuf("xc", [128, TP], F32)
        xcb = fw.sbuf("xcb", [128, TP], BF16)
        RA = [fw.sbuf("RA%d" % d, [128, TP], F32) for d in range(2)]
        IB = [fw.sbuf("IB%d" % d, [128, TP], F32) for d in range(2)]
        A2 = [fw.sbuf("A2%d" % d, [128, TP], F32) for d in range(2)]
        H = [fw.sbuf("H%d" % d, [128, TP], F32) for d in range(2)]
        GL = fw.sbuf("GL", [128, T], BF16)
        AL = fw.sbuf("AL", [128, T], BF16)
        pg_rot = Rot([fw.psum("pg%d" % i, [128, 512], F32) for i in range(4)])
        segs = self.seg_ranges()
        pieces = [(p0, min(512, TP - p0)) for p0 in range(0, TP, 512)]
        (cta, ctb), (cpa, cpb) = segs[0]
        (lta, ltb), (lpa, lpb) = segs[1]
        rv = lambda b, a0, a1: b[:, a0:a1][:, ::-1]
        for c in range(2):
            fw.op("pool", lambda e: e.memset(LX[:], 0.0), writes=[LX])
            for (ta, tb), (pa, pb) in segs:
                fw.dma("sp", LX[:, pa:pb], self.zlx.t[c * 128:(c + 1) * 128, ta:tb], LX, reads=self.p_zlx, writes=[LX])
            fw.dma("sp", GL[:], self.zgl.t[c * 128:(c + 1) * 128, :], GL, reads=self.p_zgl, writes=[GL])
            cw, cb = V["cw"], V["cb"]
            N = TP
            fw.op("dve", lambda e: e.memset(xc[:], 0.0), writes=[xc])
            fw.op("dve", lambda e, c=c: e.tensor_scalar(out=xc[:, 1:N - 2], in0=LX[:, 0:N - 3], scalar1=cw[:, 0, c:c + 1], scalar2=cb[:, c:c + 1], op0=ALU.mult, op1=ALU.add),
                  reads=[LX, cw, cb], writes=[xc])
            for k in range(1, 4):
                fw.op("dve", lambda e, c=c, k=k: e.scalar_tensor_tensor(out=xc[:, 1:N - 2], in0=LX[:, k:N - 3 + k], scalar=cw[:, k, c:c + 1], in1=xc[:, 1:N - 2],
                                                                        op0=ALU.mult, op1=ALU.add), reads=[LX, cw, xc], writes=[xc])
            fw.op("act", lambda e: e.activation(out=xcb[:], in_=xc[:], func=AF.Copy), reads=[xc], writes=[xcb])
            for (p0, pn) in pieces:
                for d in range(2):
                    pa_ = pg_rot.next()
                    px_ = pg_rot.next()
                    fw.op("pe", lambda e: e.matmul(pa_[:, 0:pn], lhsT=WA[:, d, c, :], rhs=xcb[:, p0:p0 + pn], start=True, stop=True), reads=[WA, xcb], writes=[pa_])
                    fw.op("pe", lambda e: e.matmul(px_[:, 0:pn], lhsT=WX[:, d, c, :], rhs=xcb[:, p0:p0 + pn], start=True, stop=True), reads=[WX, xcb], writes=[px_])
                    fw.op("act", lambda e: e.activation(out=RA[d][:, p0:p0 + pn], in_=pa_[:, 0:pn], func=AF.Sigmoid, bias=V["ba"][:, d, c:c + 1], scale=1.0),
                          reads=[pa_, V["ba"]], writes=[RA[d]])
                    fw.op("act", lambda e: e.activation(out=IB[d][:, p0:p0 + pn], in_=px_[:, 0:pn], func=AF.Sigmoid, bias=V["bx"][:, d, c:c + 1], scale=1.0),
                          reads=[px_, V["bx"]], writes=[IB[d]])

            def dsteps(d):
                RA_, IB_, A2_, Hd = RA[d], IB[d], A2[d], H[d]
                st = [
                    lambda: fw.op("act", lambda e: e.activation(out=RA_[:], in_=RA_[:], func=AF.Exp, scale=V["cA"][:, d, c:c + 1], bias=self.cst[:, 4:5]),
                                  reads=[RA_, V["cA"], self.cst], writes=[RA_]),
                    lambda: fw.op("pool", lambda e: e.tensor_tensor(out=A2_[:], in0=RA_[:], in1=RA_[:], op=ALU.mult), reads=[RA_], writes=[A2_]),
                    lambda: fw.op("act", lambda e: e.activation(out=A2_[:], in_=A2_[:], func=AF.Sqrt, scale=-1.0, bias=self.cst[:, 3:4]), reads=[A2_, self.cst], writes=[A2_]),
                    lambda: fw.op("pool", lambda e: e.tensor_tensor(out=IB_[:], in0=IB_[:], in1=xc[:], op=ALU.mult), reads=[IB_, xc], writes=[IB_]),
                    lambda: fw.op("dve", lambda e: e.tensor_tensor(out=IB_[:], in0=IB_[:], in1=A2_[:], op=ALU.mult), reads=[IB_, A2_], writes=[IB_]),
                ]
                if d == 0:
                    st.append(lambda: fw.op("dve", lambda e: e.tensor_tensor_scan(out=Hd[:, cpa:cpb], data0=RA_[:, cpa:cpb], data1=IB_[:, cpa:cpb], initial=0.0,
                                                                                  op0=ALU.mult, op1=ALU.add), reads=[RA_, IB_], writes=[Hd]))
                    st.append(lambda: fw.op("dve", lambda e: e.tensor_tensor_scan(out=Hd[:, lpa:lpb], data0=RA_[:, lpa:lpb], data1=IB_[:, lpa:lpb], initial=Hd[:, cpb - 1:cpb],
                                                                                  op0=ALU.mult, op1=ALU.add), reads=[RA_, IB_, Hd], writes=[Hd]))
                else:
                    st.append(lambda: fw.op("dve", lambda e: e.tensor_tensor_scan(out=rv(Hd, cpa, cpb), data0=rv(RA_, cpa, cpb), data1=rv(IB_, cpa, cpb), initial=0.0,
                                                                                  op0=ALU.mult, op1=ALU.add), reads=[RA_, IB_], writes=[Hd]))
                    st.append(lambda: fw.op("dve", lambda e: e.tensor_tensor_scan(out=rv(Hd, lpa, lpb), data0=rv(RA_, lpa, lpb), data1=rv(IB_, lpa, lpb), initial=Hd[:, cpa:cpa + 1],
                                                                                  op0=ALU.mult, op1=ALU.add), reads=[RA_, IB_, Hd], writes=[Hd]))
                return st
            for f0, f1 in zip(dsteps(0), dsteps(1)):
                f0()
                f1()
            for (ta, tb), (pa, pb) in segs:
                fw.op("pool", lambda e, pa=pa, pb=pb: e.tensor_tensor(out=H[0][:, pa:pb], in0=H[0][:, pa:pb], in1=H[1][:, pa:pb], op=ALU.add), reads=[H[0], H[1]], writes=[H[0]])
                fw.op("dve", lambda e, ta=ta, tb=tb, pa=pa, pb=pb: e.tensor_tensor(out=AL[:, ta:tb], in0=H[0][:, pa:pb], in1=GL[:, ta:tb], op=ALU.mult),
                      reads=[H[0], GL], writes=[AL])
            fw.dma("sp", self.mixd.t[c * 128:(c + 1) * 128, :], AL[:], AL, reads=[AL], writes=self.p_mixA)
        fw.pop_scope()

    def gen_b2(self, l, pst_rot):
        fw, V = self.fw, self.V
        T, TP, C = self.T, self.TP, self.C
        WA = fw.sbuf("WA", [128, 2, 2, 128], BF16)
        WX = fw.sbuf("WX", [128, 2, 2, 128], BF16)
        fw.op("pool", lambda e: e.memset(WA[:], 0.0), writes=[WA])
        fw.op("pool", lambda e: e.memset(WX[:], 0.0), writes=[WX])
        for d in range(2):
            for h in range(4):
                o = (h % 2) * 64
                fw.dma("pool", WA[o:o + 64, d, h // 2, o:o + 64], self.lru_w_a.t[l, d, h], WA, writes=[WA])
                fw.dma("pool", WX[o:o + 64, d, h // 2, o:o + 64], self.lru_w_x.t[l, d, h], WX, writes=[WX])
        yield
        LX = fw.sbuf("LXp", [128, TP], F32)
        xc = fw.sbuf("xc", [128, TP], F32)
        xcb = fw.sbuf("xcb", [128, TP], BF16)
        RA = fw.sbuf("RA", [128, TP], F32)
        IB = fw.sbuf("IB", [128, TP], F32)
        A2 = fw.sbuf("A2", [128, TP], F32)
        H1 = fw.sbuf("H1", [128, TP], F32)
        GL = fw.sbuf("GL", [128, T], BF16)
        AL = fw.sbuf("AL", [128, T], BF16)
        segs = self.seg_ranges()
        pieces = [(p0, min(512, TP - p0)) for p0 in range(0, TP, 512)]
        (cta, ctb), (cpa, cpb) = segs[0]
        (lta, ltb), (lpa, lpb) = segs[1]
        rv = lambda b, a0, a1: b[:, a0:a1][:, ::-1]
        cw, cb = V["cw"], V["cb"]
        N = TP
        for c in range(2):
            fw.op("pool", lambda e: e.memset(LX[:], 0.0), writes=[LX])
            for (ta, tb), (pa, pb) in segs:
                fw.dma("sp", LX[:, pa:pb], self.zlx.t[c * 128:(c + 1) * 128, ta:tb], LX, reads=self.p_zlx, writes=[LX])
            fw.dma("sp", GL[:], self.zgl.t[c * 128:(c + 1) * 128, :], GL, reads=self.p_zgl, writes=[GL])
            yield
            fw.op("pool", lambda e: e.memset(xc[:], 0.0), writes=[xc])
            fw.op("dve", lambda e: e.tensor_scalar(out=xc[:, 1:N - 2], in0=LX[:, 0:N - 3], scalar1=cw[:, 0, c:c + 1], scalar2=cb[:, c:c + 1], op0=ALU.mult, op1=ALU.add),
                  reads=[LX, cw, cb], writes=[xc])
            yield
            for k in range(1, 4):
                fw.op("dve", lambda e: e.scalar_tensor_tensor(out=xc[:, 1:N - 2], in0=LX[:, k:N - 3 + k], scalar=cw[:, k, c:c + 1], in1=xc[:, 1:N - 2],
                                                              op0=ALU.mult, op1=ALU.add), reads=[LX, cw, xc], writes=[xc])
                yield
            fw.op("pool", lambda e: e.tensor_copy(out=xcb[:], in_=xc[:]), reads=[xc], writes=[xcb])
            yield
            for d in range(2):
                Hd = LX if d == 0 else H1
                for (p0, pn) in pieces:
                    pg = pst_rot.next()
                    fw.op("pe", lambda e: e.matmul(pg[:, 0:pn], lhsT=WA[:, d, c, :], rhs=xcb[:, p0:p0 + pn], start=True, stop=True), reads=[WA, xcb], writes=[pg])
                    fw.op("pe", lambda e: e.matmul(pg[:, 512:512 + pn], lhsT=WX[:, d, c, :], rhs=xcb[:, p0:p0 + pn], start=True, stop=True), reads=[WX, xcb], writes=[pg])
                    fw.op("act", lambda e: e.activation(out=RA[:, p0:p0 + pn], in_=pg[:, 0:pn], func=AF.Exp, bias=V["nba"][:, d, c:c + 1], scale=-1.0),
                          reads=[pg, V["nba"]], writes=[RA])
                    fw.op("act", lambda e: e.activation(out=IB[:, p0:p0 + pn], in_=pg[:, 512:512 + pn], func=AF.Exp, bias=V["nbx"][:, d, c:c + 1], scale=-1.0),
                          reads=[pg, V["nbx"]], writes=[IB])
                    yield
                    for buf in (RA, IB):
                        fw.op("dve", lambda e: e.tensor_scalar(out=buf[:, p0:p0 + pn], in0=buf[:, p0:p0 + pn], scalar1=1.0, scalar2=None, op0=ALU.add), reads=[buf], writes=[buf])
                        fw.op("dve", lambda e: e.reciprocal(out=buf[:, p0:p0 + pn], in_=buf[:, p0:p0 + pn]), reads=[buf], writes=[buf])
                    yield
                fw.op("act", lambda e: e.activation(out=RA[:], in_=RA[:], func=AF.Exp, scale=V["cA"][:, d, c:c + 1], bias=self.cst[:, 4:5]),
                      reads=[RA, V["cA"], self.cst], writes=[RA])
                yield
                fw.op("pool", lambda e: e.tensor_tensor(out=A2[:], in0=RA[:], in1=RA[:], op=ALU.mult), reads=[RA], writes=[A2])
                yield
                fw.op("act", lambda e: e.activation(out=A2[:], in_=A2[:], func=AF.Ln, scale=-1.0, bias=self.cst[:, 3:4]), reads=[A2, self.cst], writes=[A2])
                yield
                fw.op("act", lambda e: e.activation(out=A2[:], in_=A2[:], func=AF.Exp, scale=0.5, bias=self.cst[:, 4:5]), reads=[A2, self.cst], writes=[A2])
                yield
                fw.op("pool", lambda e: e.tensor_tensor(out=IB[:], in0=IB[:], in1=xc[:], op=ALU.mult), reads=[IB, xc], writes=[IB])
                yield
                fw.op("dve", lambda e: e.tensor_tensor(out=IB[:], in0=IB[:], in1=A2[:], op=ALU.mult), reads=[IB, A2], writes=[IB])
                yield
                if d == 0:
                    fw.op("dve", lambda e: e.tensor_tensor_scan(out=Hd[:, cpa:cpb], data0=RA[:, cpa:cpb], data1=IB[:, cpa:cpb], initial=0.0,
                                                                op0=ALU.mult, op1=ALU.add), reads=[RA, IB], writes=[Hd])
                    fw.op("dve", lambda e: e.tensor_tensor_scan(out=Hd[:, lpa:lpb], data0=RA[:, lpa:lpb], data1=IB[:, lpa:lpb], initial=Hd[:, cpb - 1:cpb],
                                                                op0=ALU.mult, op1=ALU.add), reads=[RA, IB, Hd], writes=[Hd])
                else:
                    fw.op("dve", lambda e: e.tensor_tensor_scan(out=rv(Hd, cpa, cpb), data0=rv(RA, cpa, cpb), data1=rv(IB, cpa, cpb), initial=0.0,
                                                                op0=ALU.mult, op1=ALU.add), reads=[RA, IB], writes=[Hd])
                    fw.op("dve", lambda e: e.tensor_tensor_scan(out=rv(Hd, lpa, lpb), data0=rv(RA, lpa, lpb), data1=rv(IB, lpa, lpb), initial=Hd[:, cpa:cpa + 1],
                                                                op0=ALU.mult, op1=ALU.add), reads=[RA, IB, Hd], writes=[Hd])
                yield
            for (ta, tb), (pa, pb) in segs:
                fw.op("pool", lambda e: e.tensor_tensor(out=LX[:, pa:pb], in0=LX[:, pa:pb], in1=H1[:, pa:pb], op=ALU.add), reads=[LX, H1], writes=[LX])
                fw.op("dve", lambda e: e.tensor_tensor(out=AL[:, ta:tb], in0=LX[:, pa:pb], in1=GL[:, ta:tb], op=ALU.mult), reads=[LX, GL], writes=[AL])
                yield
            fw.dma("sp", self.mixd.t[c * 128:(c + 1) * 128, :], AL[:], AL, reads=[AL], writes=self.p_mixA)
            yield

    def phase_c(self, l, last, bg=None):
        fw = self.fw
        T, C, L = self.T, self.C, self.L
        NKC = T // 128
        fw.push_scope()
        sets = []
        for par in range(2):
            kTh = fw.sbuf("kTh%d" % par, [128, T], BF16)
            qTh = fw.sbuf("qTh%d" % par, [128, T], BF16)
            fw.op("pool", lambda e, kTh=kTh: e.memset(kTh[:], 0.0), writes=[kTh])
            fw.op("pool", lambda e, qTh=qTh: e.memset(qTh[:], 0.0), writes=[qTh])
            Va = fw.sbuf("Va%d" % par, [128, NKC, 128], BF16)
            fw.op("pool", lambda e, Va=Va: e.memset(Va[:], 0.0), writes=[Va])
            oc = 64 if par == 0 else 0
            fw.op("pool", lambda e, Va=Va, oc=oc: e.memset(Va[:, :, oc:oc + 1], 1.0), writes=[Va])
            sets.append((kTh, qTh, Va))
        pst_rot = Rot([fw.psum("pst%d" % i, [128, 1024], F32) for i in range(3)])
        po_rot = Rot([fw.psum("po%d" % i, [128, 512], F32) for i in range(2)])
        pt_rot = Rot([fw.sbuf("pt%d" % i, [128, 1024], BF16) for i in range(4)])
        rden_rot = Rot([fw.sbuf("rden%d" % i, [128, 512], F32) for i in range(2)])
        bcs = fw.sbuf("bcs", [128, 512], F32)
        ast_rot = Rot([fw.sbuf("ast%d" % i, [128, 512], BF16) for i in range(3)])
        if bg:
            bg = self.gen_b2(l, pst_rot)
            if DRAIN_FIRST:
                for _ in bg:
                    pass

        def load(h):
            kTh, qTh, Va = sets[h % 2]
            vo = 0 if h % 2 == 0 else 64
            fw.dma("sp", kTh[0:96, :], self.kT.t[h], kTh, reads=self.p_kT, writes=[kTh])
            fw.dma("sp", qTh[0:96, :], self.qT.t[h], qTh, reads=self.p_qT, writes=[qTh])
            for k0 in range(0, NKC, 4):
                k1 = min(NKC, k0 + 4)
                fw.dma("sp", Va[:, k0:k1, vo:vo + 64], self.vv.t[k0 * 128:k1 * 128, h * 64:(h + 1) * 64].rearrange("(kc p) d -> p kc d", p=128), Va,
                       reads=self.p_vv, writes=[Va])

        items = []
        for h in range(NH):
            for gi, (t0, n, seg) in enumerate(self.groups):
                if seg == 1 and last:
                    continue
                npairs = (C // 128 if seg == 1 else NKC) // 2
                for j in range(npairs):
                    items.append((h, gi, j, npairs))
        state = {"loaded": -1, "po": None, "defer": []}

        def ensure_loaded(h):
            while state["loaded"] < min(h, NH - 1):
                state["loaded"] += 1
                load(state["loaded"])

        def ST(it):
            h, gi, j, npairs = it
            ensure_loaded(h)
            kTh, qTh, Va = sets[h % 2]
            t0, n, seg = self.groups[gi]
            pst = pst_rot.next()
            pt = pt_rot.next()
            if FILL > 0:
                nf = min(FILL, n)
                fw.op("pe", lambda e: e.matmul(pst[:, 0:nf], lhsT=kTh[:, 2 * j * 128:(2 * j + 1) * 128], rhs=qTh[:, t0:t0 + nf], start=True, stop=True),
                      reads=[kTh, qTh], writes=[pst])
            for u in range(2):
                kc = 2 * j + u
                fw.op("pe", lambda e: e.matmul(pst[:, u * 512:u * 512 + n], lhsT=kTh[:, kc * 128:(kc + 1) * 128], rhs=qTh[:, t0:t0 + n], start=True, stop=True),
                      reads=[kTh, qTh], writes=[pst])
            v3 = lambda b: b[:, :].rearrange("p (u t) -> p u t", u=2)[:, :, 0:n]
            fw.op("act", lambda e: e.activation(out=v3(pt), in_=v3(pst), func=AF.Exp), reads=[pst], writes=[pt])
            return pt

        def PV(it, pt):
            h, gi, j, npairs = it
            kTh, qTh, Va = sets[h % 2]
            t0, n, seg = self.groups[gi]
            if j == 0:
                state["po"] = po_rot.next()
                state["rden"] = rden_rot.next()
                keep = []
                for ent in state["defer"]:
                    if ent[2] is state["po"]:
                        ent[1]()
                    else:
                        keep.append(ent)
                state["defer"] = keep
            po = state["po"]
            for u in range(2):
                kc = 2 * j + u
                fw.op("pe", lambda e: e.matmul(po[:, 0:n], lhsT=Va[:, kc, :], rhs=pt[:, u * 512:u * 512 + n], start=(kc == 0), stop=(kc == 2 * npairs - 1)),
                      reads=[Va, pt], writes=[po])
            if j == npairs - 1:
                dp = 64 if h % 2 == 0 else 0
                op_ = 0 if h % 2 == 0 else 64
                rden = state["rden"]

                fw.op("dve", lambda e: e.reciprocal(out=rden[dp:dp + 1, 0:n], in_=po[dp:dp + 1, 0:n]), reads=[po], writes=[rden])
                box = {}

                def epi2():
                    box["pbc"] = pst_rot.next()
                    fw.op("pe", lambda e: e.matmul(box["pbc"][:, 0:n], lhsT=self.onesf[dp:dp + 1, :], rhs=rden[dp:dp + 1, 0:n], start=True, stop=True),
                          reads=[self.onesf, rden], writes=[box["pbc"]])

                def epi3():
                    pbc = box["pbc"]
                    fw.op("dve", lambda e: e.tensor_copy(out=bcs[op_:op_ + 64, 0:n], in_=pbc[op_:op_ + 64, 0:n]), reads=[pbc], writes=[bcs])
                    ast = ast_rot.next()
                    fw.op("dve", lambda e: e.tensor_tensor(out=ast[op_:op_ + 64, 0:n], in0=po[op_:op_ + 64, 0:n], in1=bcs[op_:op_ + 64, 0:n], op=ALU.mult),
                          reads=[po, bcs], writes=[ast])
                    fw.dma("sp", self.mixd.t[256 + h * 64:256 + (h + 1) * 64, t0:t0 + n], ast[op_:op_ + 64, 0:n], ast, reads=[ast], writes=[self.p_mixB[gi]])
                state["defer"].append([5, lambda: (epi2(), epi3()), po])

        pts = {}
        LOOK = 2

        def run_deferred(force=False):
            keep = []
            for ent in state["defer"]:
                ent[0] -= 1
                if ent[0] <= 0 or force:
                    ent[1]()
                else:
                    keep.append(ent)
            state["defer"] = keep

        for i in range(min(LOOK, len(items))):
            pts[i] = ST(items[i])
        for i, it in enumerate(items):
            if i + LOOK < len(items):
                pts[i + LOOK] = ST(items[i + LOOK])
            PV(it, pts.pop(i))
            ensure_loaded(it[0] + 1)
            run_deferred()
            if bg and i % BG_STRIDE == 0:
                next(bg, None)
        while state["defer"]:
            run_deferred(force=True)
        if bg:
            for _ in bg:
                pass
        fw.pop_scope()

    def phase_d1(self, l, last):
        fw, V = self.fw, self.V
        fw.push_scope()
        WO = fw.sbuf("WO", [128, KD, D], BF16)
        for k in range(KD):
            fw.dma("pool", WO[:, k, :], self.w_out.t[l, k * 128:(k + 1) * 128, :], WO, writes=[WO])
        G1b = [fw.sbuf("G1b%d" % seg, [128, D], F32) for seg in range(2)]
        for seg in range(2):
            fw.dma("sp", G1b[seg][:], self.modv.t[l, seg, 2 * D:3 * D].unsqueeze(0).to_broadcast([128, D]), G1b[seg], reads=[self.p_modv[l]], writes=[G1b[seg]])
        msets = [fw.sbuf("mix%d" % i, [128, KD, 512], BF16) for i in range(2)]
        xsets = [[fw.sbuf("xt%d_%d" % (i, s), [128, D], F32) for s in range(4)] for i in range(2)]
        tmp_rot = Rot([fw.sbuf("tmp%d" % i, [128, D], F32) for i in range(2)])
        x1S = [[fw.sbuf("x1_%d_%d" % (i, s), [128, D], F32) for s in range(4)] for i in range(2)]
        xn = [fw.sbuf("xn%d" % s, [128, D], BF16) for s in range(4)]
        junk = fw.sbuf("junk", [128, D], BF16)
        ss = fw.sbuf("ss", [128, 4], F32)
        rp = fw.sbuf("rp", [128, 4], F32)
        hT_rot = Rot([fw.sbuf("h2T%d" % i, [128, KD, 512], BF16) for i in range(2)])
        py_rot = Rot([fw.psum("py%d" % i, [128, 1024], F32) for i in range(2)])
        ptr_rot = Rot([fw.psum("ptr%d" % i, [128, 1024], BF16) for i in range(2)])
        gl = [gi for gi, g in enumerate(self.groups) if not (last and g[2] == 1)]

        def load(ii):
            gi = gl[ii]
            t0, n, seg = self.groups[gi]
            mx = msets[ii % 2]
            fw.dma("sp", mx[:, :, 0:n], self.mixd.t[:, t0:t0 + n].rearrange("(k p) t -> p k t", p=128), mx,
                   reads=[self.p_mixA[gi], self.p_mixB[gi], self.p_mixC[gi]], writes=[mx])
            xs = xsets[ii % 2]
            rd = [self.p_xres[gi]] if l > 0 else []
            for s in range(n // 128):
                fw.dma("sp", xs[s][:], self.res_src(l, t0 + s * 128, 128), xs[s], reads=rd, writes=[xs[s]])

        def st1(ii):
            gi = gl[ii]
            t0, n, seg = self.groups[gi]
            mx, xs, x1 = msets[ii % 2], xsets[ii % 2], x1S[ii % 2]
            for s in range(n // 128):
                py = py_rot.next()
                for hf in range(2):
                    for k in range(KD):
                        fw.op("pe", lambda e, k=k, hf=hf, py=py, s=s: e.matmul(py[:, hf * 512:(hf + 1) * 512], lhsT=mx[:, k, s * 128:(s + 1) * 128],
                                                                              rhs=WO[:, k, hf * 512:(hf + 1) * 512], start=(k == 0), stop=(k == KD - 1)),
                              reads=[mx, WO], writes=[py])
                tmp = tmp_rot.next()
                fw.op("dve", lambda e, py=py, tmp=tmp: e.tensor_tensor(out=tmp[:], in0=py[:], in1=G1b[seg][:], op=ALU.mult), reads=[py, G1b[seg]], writes=[tmp])
                fw.op("pool", lambda e, s=s, tmp=tmp: e.tensor_tensor(out=x1[s][:], in0=xs[s][:], in1=tmp[:], op=ALU.add), reads=[xs[s], tmp], writes=[x1[s]])
                fw.dma("pool", self.x1d.t[t0 + s * 128:t0 + (s + 1) * 128, :], x1[s][:], x1[s], reads=[x1[s]], writes=[self.p_x1d[gi]])

        def stN(ii):
            gi = gl[ii]
            t0, n, seg = self.groups[gi]
            self.norm_part(x1S[ii % 2], n // 128, junk, ss, rp, xn)

        def stT(ii):
            gi = gl[ii]
            t0, n, seg = self.groups[gi]
            hT = hT_rot.next()
            self.transpose_part(xn, n // 128, V["G2", seg], V["SH2", seg], hT, ptr_rot)
            fw.dma("sp", self.h2d.t[:, t0:t0 + n].rearrange("(k p) t -> p k t", p=128), hT[:, :, 0:n], hT, reads=[hT], writes=[self.p_h2d[gi]])

        NGL = len(gl)
        load(0)
        if NGL > 1:
            load(1)
        st1(0)
        for ii in range(NGL):
            stN(ii)
            if ii + 1 < NGL:
                st1(ii + 1)
            if ii + 2 < NGL:
                load(ii + 2)
            stT(ii)
        fw.pop_scope()

    def phase_d2(self, l, last):
        fw, V = self.fw, self.V
        C = self.C
        fw.push_scope()
        W1 = fw.sbuf("W1", [128, KD, DFF], BF16)
        W2 = fw.sbuf("W2", [128, 32, D], BF16)
        for k in range(KD):
            fw.dma("pool", W1[:, k, :], self.w_ff1.t[l, k * 128:(k + 1) * 128, :], W1, writes=[W1])
        for kq in range(4):
            fw.dma("pool", W2[:, kq * 8:(kq + 1) * 8, :], self.w_ff2.t[l, kq * 1024:(kq + 1) * 1024, :].rearrange("(k p) n -> p k n", p=128), W2, writes=[W2])
        G2b = [fw.sbuf("G2b%d" % seg, [128, D], F32) for seg in range(2)]
        for seg in range(2):
            fw.dma("sp", G2b[seg][:], self.modv.t[l, seg, 5 * D:6 * D].unsqueeze(0).to_broadcast([128, D]), G2b[seg], reads=[self.p_modv[l]], writes=[G2b[seg]])
        hsets = [fw.sbuf("h2T%d" % i, [128, KD, 512], BF16) for i in range(2)]
        uT = fw.sbuf("uT", [128, 32, 512], BF16)
        rl_rot = Rot([fw.sbuf("rl%d" % i, [128, 512], F32) for i in range(3)])
        x1_rot = Rot([fw.sbuf("x1t%d" % i, [128, D], F32) for i in range(2)])
        tmp_rot = Rot([fw.sbuf("tmp%d" % i, [128, D], F32) for i in range(1)])
        pu_rot = Rot([fw.psum("pu%d" % i, [128, 512], F32) for i in range(4)])
        py_rot = Rot([fw.psum("py%d" % i, [128, 1024], F32) for i in range(2)])
        gl = [gi for gi, g in enumerate(self.groups) if not (last and g[2] == 1)]

        def load(gi):
            t0, n, seg = self.groups[gi]
            fw.dma("sp", hsets[gi % 2][:, :, 0:n], self.h2d.t[:, t0:t0 + n].rearrange("(k p) t -> p k t", p=128), hsets[gi % 2], reads=[self.p_h2d[gi]], writes=[hsets[gi % 2]])

        load(gl[0])
        for ii, gi in enumerate(gl):
            t0, n, seg = self.groups[gi]
            if ii + 1 < len(gl):
                load(gl[ii + 1])
            nsub = n // 128
            hT = hsets[gi % 2]
            for oc in range(32):
                pu = pu_rot.next()
                rl = rl_rot.next()
                for k in range(KD):
                    fw.op("pe", lambda e, k=k, oc=oc, pu=pu: e.matmul(pu[:, 0:n], lhsT=W1[:, k, oc * 128:(oc + 1) * 128], rhs=hT[:, k, 0:n], start=(k == 0), stop=(k == KD - 1)),
                          reads=[W1, hT], writes=[pu])
                if oc % 2 == 0:
                    fw.op("act", lambda e, pu=pu, rl=rl: e.activation(out=rl[:, 0:n], in_=pu[:, 0:n], func=AF.Relu), reads=[pu], writes=[rl])
                    fw.op("dve", lambda e, oc=oc, rl=rl: e.tensor_tensor(out=uT[:, oc, 0:n], in0=rl[:, 0:n], in1=rl[:, 0:n], op=ALU.mult), reads=[rl], writes=[uT])
                else:
                    fw.op("dve", lambda e, pu=pu, rl=rl: e.tensor_scalar(out=rl[:, 0:n], in0=pu[:, 0:n], scalar1=0.0, scalar2=None, op0=ALU.max), reads=[pu], writes=[rl])
                    fw.op("pool", lambda e, oc=oc, rl=rl: e.tensor_tensor(out=uT[:, oc, 0:n], in0=rl[:, 0:n], in1=rl[:, 0:n], op=ALU.mult), reads=[rl], writes=[uT])
            for s in range(nsub):
                x1t = x1_rot.next()
                fw.dma("sp", x1t[:], self.x1d.t[t0 + s * 128:t0 + (s + 1) * 128, :], x1t, reads=[self.p_x1d[gi]], writes=[x1t])
                py = py_rot.next()
                for hf in range(2):
                    for kc in range(32):
                        fw.op("pe", lambda e, kc=kc, hf=hf, py=py, s=s: e.matmul(py[:, hf * 512:(hf + 1) * 512], lhsT=uT[:, kc, s * 128:(s + 1) * 128],
                                                                                rhs=W2[:, kc, hf * 512:(hf + 1) * 512], start=(kc == 0), stop=(kc == 31)),
                              reads=[uT, W2], writes=[py])
                tmp = tmp_rot.next()
                fw.op("dve", lambda e, py=py, tmp=tmp: e.tensor_tensor(out=tmp[:], in0=py[:], in1=G2b[seg][:], op=ALU.mult), reads=[py, G2b[seg]], writes=[tmp])
                fw.op("pool", lambda e, x1t=x1t, tmp=tmp: e.tensor_tensor(out=x1t[:], in0=x1t[:], in1=tmp[:], op=ALU.add), reads=[x1t, tmp], writes=[x1t])
                r0 = t0 + s * 128
                if last:
                    fw.dma("sp", self.out.t[r0 - C:r0 - C + 128, :], x1t[:], x1t, reads=[x1t], writes=[self.p_out[gi]])
                else:
                    fw.dma("sp", self.xres.t[r0:r0 + 128, :], x1t[:], x1t, reads=[x1t], writes=[self.p_xres[gi]])
        fw.pop_scope()


def const_tables(L, C):
    T = L + C
    ropeC = np.ones((T, 32), np.float32)
    ropeS = np.zeros((T, 32), np.float32)
    t = np.arange(L)
    pos = np.stack([(t // GRID_W).astype(np.float32), (t % GRID_W).astype(np.float32)], axis=-1)
    freqs = np.power(np.float32(10000.0), -np.arange(8, dtype=np.float32) / np.float32(8)).astype(np.float32)
    ang = (pos[:, :, None] * freqs).astype(np.float32)
    cs, sn = np.cos(ang).astype(np.float32), np.sin(ang).astype(np.float32)
    cb = np.stack([cs, cs], axis=2)
    ssg = np.stack([-sn, sn], axis=2)
    ropeC[C:] = cb.reshape(L, 32)
    ropeS[C:] = ssg.reshape(L, 32)
    wins = {(0, 0): 2, (0, 1): 4, (1, 0): 8, (1, 1): 16}
    pcorr = np.ones((128, 2, 2, 8), np.float32)
    pinvw = np.ones((128, 2), np.float32)
    for (c, hf), w in wins.items():
        ps = slice(hf * 64, (hf + 1) * 64)
        pinvw[ps, c] = 1.0 / w
        for j in range(8):
            cntl = (j + w // 2) - max(j - w // 2, 0)
            pcorr[ps, c, 0, j] = w / cntl
            tt = j - 8
            cntr = min(tt + w // 2, 0) - (tt - w // 2)
            pcorr[ps, c, 1, j] = w / cntr
    ident = np.eye(128, dtype=np.float32).astype(ml_dtypes.bfloat16)
    return dict(ropeC=ropeC, ropeS=ropeS, pcorr=pcorr, pinvw=pinvw, ident=ident)


_W_NAMES = ["w_mod", "b_mod", "g_norm1", "g_norm2", "w_in", "conv_w", "conv_b", "lru_w_a", "lru_b_a", "lru_w_x", "lru_b_x",
            "lru_lambda", "g_q_lat", "w_uq", "g_kv_lat", "w_ukv", "g_qn", "g_kn", "w_pool", "pool_scale", "w_out", "w_ff1", "w_ff2"]


def make_in_maps(inputs, L, C, NL):
    tabs = const_tables(L, C)
    B = inputs["x"].shape[0]
    shared = {k: np.ascontiguousarray(np.asarray(inputs[k], np.float32)[:NL]) for k in _W_NAMES}
    shared.update(tabs)
    maps = []
    for b in range(B):
        m = dict(shared)
        m["x"] = np.ascontiguousarray(inputs["x"][b], dtype=np.float32)
        m["ctx"] = np.ascontiguousarray(inputs["ctx"][b], dtype=np.float32)
        m["cvec"] = np.ascontiguousarray(np.stack([inputs["c"][b], inputs["c_ctx"]], axis=0), dtype=np.float32)
        maps.append(m)
    return maps


_CACHE = {}


def kernel(**inputs):
    x = np.asarray(inputs["x"])
    B, L, _ = x.shape
    C = np.asarray(inputs["ctx"]).shape[1]
    NL = np.asarray(inputs["w_in"]).shape[0]
    key = (L, C, NL)
    if key not in _CACHE:
        _CACHE[key] = Prog(L, C, NL).build()
    nc = _CACHE[key]
    inputs = {k: np.asarray(v) for k, v in inputs.items()}
    maps = make_in_maps(inputs, L, C, NL)
    res = run_bass_kernel_spmd(nc, maps, core_ids=list(range(B)))
    return np.stack([np.asarray(r["out"], dtype=np.float32) for r in res.results], axis=0)
```

```python
import math
import numpy as np
import ml_dtypes
from contextlib import ExitStack
import concourse.bass as bass
import concourse.mybir as mybir
from concourse.bass_utils import run_bass_kernel_spmd

F32 = mybir.dt.float32
BF16 = mybir.dt.bfloat16
AF = mybir.ActivationFunctionType
ALU = mybir.AluOpType
AX = mybir.AxisListType

D = 1024
KD = 8
DIN = 1440
DFF = 4096
NH = 8
DQK = 96
EPS = 1e-6
PAD = 8
GRID_W = 64
OVERLAP_B2 = True
DRAIN_FIRST = False
BG_STRIDE = 6
FILL = 0


class Buf:
    __slots__ = ("t", "w", "r", "sem", "gsem", "name")

    def __init__(self, t=None, name=""):
        self.t = t
        self.w = None
        self.r = []
        self.sem = None
        self.gsem = None
        self.name = name

    def __getitem__(self, k):
        return self.t[k]


class Rot:
    def __init__(self, bufs):
        self.bufs = bufs
        self.i = 0

    def next(self):
        b = self.bufs[self.i % len(self.bufs)]
        self.i += 1
        return b


class FW:
    def __init__(self, nc, es):
        self.nc = nc
        self.es = es
        self.scopes = []
        self.engs = {"pe": nc.tensor, "act": nc.scalar, "dve": nc.vector, "pool": nc.gpsimd, "sp": nc.sync}
        self.sems = {}
        self.cnt = {}
        self.waited = {k: {} for k in self.engs}
        for k in self.engs:
            self.sems[k] = es.enter_context(nc.semaphore("s_" + k))
            self.cnt[k] = 0
        self.sems["bar"] = es.enter_context(nc.semaphore("s_bar"))
        self.cnt["bar"] = 0
        self.nsem = 0
        self.free_sems = []
        self.free_gsems = []
        self.scope_bufs = []
        self.uid = 0

    def _name(self, name):
        self.uid += 1
        return "%s_%d" % (name, self.uid)

    def sbuf(self, name, shape, dt):
        b = Buf(self.es.enter_context(self.nc.sbuf_tensor(self._name(name), shape, dt)), name)
        if self.scope_bufs:
            self.scope_bufs[-1].append(b)
        return b

    def psum(self, name, shape, dt=F32):
        b = Buf(self.es.enter_context(self.nc.psum_tensor(self._name(name), shape, dt)), name)
        if self.scope_bufs:
            self.scope_bufs[-1].append(b)
        return b

    def dram(self, name, shape, dt, kind="Internal"):
        return Buf(self.nc.dram_tensor(name, shape, dt, kind=kind), name)

    def push_scope(self):
        self.scopes.append(self.es)
        self.es = ExitStack()
        self.es.__enter__()
        self.scope_bufs.append([])

    def pop_scope(self):
        self.barrier()
        for b in self.scope_bufs.pop():
            if b.sem is not None:
                self.free_sems.append(b.sem)
                b.sem = None
            if b.gsem is not None:
                self.free_gsems.append(b.gsem)
                b.gsem = None
        self.es.__exit__(None, None, None)
        self.es = self.scopes.pop()

    def _wait(self, e, s, v):
        wd = self.waited[e]
        if wd.get(s, 0) < v:
            self.engs[e].wait_ge(self.sems[s], v)
            wd[s] = v

    def _waits(self, e, reads, writes):
        deps = {}
        for b in reads:
            if b.w is not None:
                s, v = b.w
                if deps.get(s, 0) < v:
                    deps[s] = v
        for b in writes:
            if b.w is not None:
                s, v = b.w
                if deps.get(s, 0) < v:
                    deps[s] = v
            for (s, v) in b.r:
                if deps.get(s, 0) < v:
                    deps[s] = v
        for s, v in deps.items():
            if e == "pe" and s == "pe":
                continue
            self._wait(e, s, v)

    def _mark(self, tok, reads, writes):
        for b in reads:
            b.r.append(tok)
            if len(b.r) > 64:
                mx = {}
                for (s, v) in b.r:
                    if mx.get(s, 0) < v:
                        mx[s] = v
                b.r = list(mx.items())
        for b in writes:
            b.w = tok
            b.r = []

    def op(self, e, fn, reads=(), writes=()):
        self._waits(e, reads, writes)
        ins = fn(self.engs[e])
        self.cnt[e] += 1
        ins.then_inc(self.sems[e], 1)
        self._mark((e, self.cnt[e]), reads, writes)
        return ins

    def dma(self, q, out, in_, sb, reads=(), writes=(), **kw):
        self._waits(q, reads, writes)
        attr, pool_, pre = ("gsem", self.free_gsems, "g") if q == "pool" else ("sem", self.free_sems, "d")
        if getattr(sb, attr) is None:
            if pool_:
                setattr(sb, attr, pool_.pop())
            else:
                key = "%s%d" % (pre, self.nsem)
                self.nsem += 1
                self.sems[key] = self.scopes[0].enter_context(self.nc.semaphore(key)) if self.scopes else \
                    self.es.enter_context(self.nc.semaphore(key))
                self.cnt[key] = 0
                setattr(sb, attr, key)
        key = getattr(sb, attr)
        ins = self.engs[q].dma_start(out=out, in_=in_, **kw)
        self.cnt[key] += 16
        ins.then_inc(self.sems[key], 16)
        self._mark((key, self.cnt[key]), reads, writes)
        return ins

    def barrier(self):
        for s, v in self.cnt.items():
            if s in ("sp", "bar") or v == 0:
                continue
            self._wait("sp", s, v)
        self.engs["sp"].sem_inc(self.sems["bar"], 1)
        self.cnt["bar"] += 1
        for e in ("pe", "act", "dve", "pool"):
            self._wait(e, "bar", self.cnt["bar"])

    def finish(self):
        for s, v in self.cnt.items():
            if s in ("sp", "bar") or v == 0:
                continue
            self._wait("sp", s, v)


class Prog:
    def __init__(self, L, C, NL, debug=False):
        self.L, self.C, self.NL, self.debug = L, C, NL, debug
        self.T = L + C
        self.TP = self.T + 3 * PAD
        self.groups = []
        t = 0
        while t < C:
            n = min(512, C - t)
            self.groups.append((t, n, 1))
            t += n
        while t < self.T:
            n = min(512, self.T - t)
            self.groups.append((t, n, 0))
            t += n
        self.NG = len(self.groups)

    def ppos(self, t):
        return PAD + t if t < self.C else 2 * PAD + t

    def build(self):
        nc = bass.Bass("TRN2", target_bir_lowering=False)
        self.nc = nc
        L, C, T, NL = self.L, self.C, self.T, self.NL
        with ExitStack() as es:
            fw = FW(nc, es)
            self.fw = fw
            I = lambda name, shape, dt=F32: fw.dram(name, shape, dt, kind="ExternalInput")
            self.x = I("x", [L, D])
            self.ctx = I("ctx", [C, D])
            self.cvec = I("cvec", [2, D])
            self.w_mod = I("w_mod", [NL, D, 6 * D])
            self.b_mod = I("b_mod", [NL, 6 * D])
            self.g_norm1 = I("g_norm1", [NL, D])
            self.g_norm2 = I("g_norm2", [NL, D])
            self.w_in = I("w_in", [NL, D, DIN])
            self.conv_w = I("conv_w", [NL, 4, 256])
            self.conv_b = I("conv_b", [NL, 256])
            self.lru_w_a = I("lru_w_a", [NL, 2, 4, 64, 64])
            self.lru_b_a = I("lru_b_a", [NL, 2, 256])
            self.lru_w_x = I("lru_w_x", [NL, 2, 4, 64, 64])
            self.lru_b_x = I("lru_b_x", [NL, 2, 256])
            self.lru_lambda = I("lru_lambda", [NL, 2, 256])
            self.g_q_lat = I("g_q_lat", [NL, 384])
            self.w_uq = I("w_uq", [NL, 384, 768])
            self.g_kv_lat = I("g_kv_lat", [NL, 256])
            self.w_ukv = I("w_ukv", [NL, 256, 1024])
            self.g_qn = I("g_qn", [NL, 96])
            self.g_kn = I("g_kn", [NL, 96])
            self.w_pool = I("w_pool", [NL, 4, 64, 64])
            self.pool_scale = I("pool_scale", [NL, 256])
            self.w_out = I("w_out", [NL, D, D])
            self.w_ff1 = I("w_ff1", [NL, D, DFF])
            self.w_ff2 = I("w_ff2", [NL, DFF, D])
            self.ident_d = I("ident", [128, 128], BF16)
            self.ropeC = I("ropeC", [T, 32])
            self.ropeS = I("ropeS", [T, 32])
            self.pcorr = I("pcorr", [128, 2, 2, 8])
            self.pinvw = I("pinvw", [128, 2])
            self.out = fw.dram("out", [L, D], F32, kind="ExternalOutput")

            skind = "ExternalOutput" if self.debug else "Internal"
            S = lambda name, shape, dt: fw.dram(name, shape, dt, kind=skind)
            NG = self.NG
            self.modv = S("modv", [NL, 2, 6 * D], F32)
            self.zlx = S("zlx", [256, T], F32)
            self.zgl = S("zgl", [256, T], BF16)
            self.zpu = S("zpu", [256, T], F32)
            self.zq = S("zq", [672, T], BF16)
            self.qT = S("qT", [NH, DQK, T], BF16)
            self.kT = S("kT", [NH, DQK, T], BF16)
            self.vv = S("vv", [T, 512], BF16)
            self.mixd = S("mixd", [D, T], BF16)
            self.x1d = S("x1d", [T, D], F32)
            self.h2d = S("h2d", [D, T], BF16)
            self.xres = S("xres", [T, D], F32)
            P = lambda nm: [Buf(None, "%s%d" % (nm, g)) for g in range(NG)]
            self.p_modv = [Buf(None, "modv%d" % l) for l in range(NL)]
            self.p_zlx, self.p_zgl, self.p_zpu, self.p_zq = P("zlx"), P("zgl"), P("zpu"), P("zq")
            self.p_qT, self.p_kT, self.p_vv = P("qT"), P("kT"), P("vv")
            self.p_mixA, self.p_mixB, self.p_mixC = P("mixA"), P("mixB"), P("mixC")
            self.p_x1d, self.p_h2d, self.p_xres = P("x1d"), P("h2d"), P("xres")
            self.p_out = P("out")

            self.idb = fw.sbuf("idb", [128, 128], BF16)
            fw.dma("sp", self.idb[:], self.ident_d[:], self.idb, writes=[self.idb])
            self.onesb = fw.sbuf("onesb", [128, 128], BF16)
            fw.op("dve", lambda e: e.memset(self.onesb[:], 1.0), writes=[self.onesb])
            self.onesf = fw.sbuf("onesf", [128, 128], F32)
            fw.op("dve", lambda e: e.memset(self.onesf[:], 1.0), writes=[self.onesf])
            self.cst = fw.sbuf("cst", [128, 8], F32)
            for j, v in enumerate((1024 * EPS, EPS, 96 * EPS, 1.0, 0.0)):
                fw.op("dve", lambda e, j=j, v=v: e.memset(self.cst[:, j:j + 1], v), writes=[self.cst])
            self.colmod = [fw.sbuf("colmod%d" % l, [128, 48, 2], F32) for l in range(NL)]

            self.marks = []
            mark = lambda nm: self.marks.append((nm, dict(fw.cnt)))
            mark("mod")
            self.phase_mod()
            for l in range(NL):
                last = (l == NL - 1)
                mark("vec%d" % l)
                self.phase_vec(l)
                mark("a1%d" % l)
                self.phase_a1(l)
                mark("a2%d" % l)
                self.phase_a2(l)
                mark("b1%d" % l)
                if not OVERLAP_B2:
                    self.phase_b1(l)
                mark("b2%d" % l)
                if not OVERLAP_B2:
                    self.phase_b2(l)
                mark("c%d" % l)
                self.phase_c(l, last, bg=OVERLAP_B2)
                mark("d1%d" % l)
                self.phase_d1(l, last)
                mark("d2%d" % l)
                self.phase_d2(l, last)
                self.end_vec()
            mark("end")
            fw.finish()
        return nc

    def col_load(self, q, dst_buf, dst_ap, src_ap, reads=()):
        self.fw.dma(q, dst_ap, src_ap, dst_buf, reads=list(reads), writes=[dst_buf], allow_slow_non_contiguous=True)

    def phase_mod(self):
        fw, NL = self.fw, self.NL
        fw.push_scope()
        cact = fw.sbuf("cact", [128, 8, 2], F32)
        for j in range(2):
            self.col_load("sp", cact, cact[:, :, j], self.cvec.t[j].rearrange("(k p) -> p k", p=128))
        fw.op("act", lambda e: e.activation(out=cact[:], in_=cact[:], func=AF.Silu), reads=[cact], writes=[cact])
        wrot = Rot([fw.sbuf("wm%d" % i, [128, 1536], F32) for i in range(4)])
        psm = [fw.psum("psm%d" % i, [128, 512], F32) for i in range(3)]
        pcol = fw.psum("pcol", [128, 96], F32)
        id2 = fw.sbuf("id2", [2, 2], F32)
        fw.op("dve", lambda e: e.memset(id2[:], 0.0), writes=[id2])
        fw.op("dve", lambda e: e.memset(id2[0:1, 0:1], 1.0), writes=[id2])
        fw.dma("sp", id2[1:2, 1:2], self.onesf[0:1, 0:1], id2, reads=[self.onesf], writes=[id2])
        for l in range(NL):
            bm = fw.sbuf("bm", [2, 6 * D], F32)
            fw.dma("sp", bm[:], self.b_mod.t[l].unsqueeze(0).to_broadcast([2, 6 * D]), bm, writes=[bm])
            msb = fw.sbuf("msb", [2, 6 * D], F32)
            for nq in range(4):
                for k in range(KD):
                    wt = wrot.next()
                    fw.dma("sp", wt[:], self.w_mod.t[l, k * 128:(k + 1) * 128, nq * 1536:(nq + 1) * 1536], wt, writes=[wt])
                    for j in range(3):
                        fw.op("pe", lambda e, j=j, k=k, wt=wt: e.matmul(psm[j][0:2, :], lhsT=cact[:, k, :], rhs=wt[:, j * 512:(j + 1) * 512],
                                                                       start=(k == 0), stop=(k == KD - 1)),
                              reads=[cact, wt], writes=[psm[j]])
                for j in range(3):
                    c0 = nq * 1536 + j * 512
                    fw.op("dve", lambda e, j=j, c0=c0: e.tensor_tensor(out=msb[:, c0:c0 + 512], in0=psm[j][0:2, :], in1=bm[:, c0:c0 + 512], op=ALU.add),
                          reads=[psm[j], bm], writes=[msb])
            fw.dma("sp", self.modv.t[l], msb[:], msb, reads=[msb], writes=[self.p_modv[l]])
            for ck in range(48):
                fw.op("pe", lambda e, ck=ck: e.matmul(pcol[:, 2 * ck:2 * ck + 2], lhsT=msb[0:2, ck * 128:(ck + 1) * 128], rhs=id2[:, :],
                                                      start=True, stop=True), reads=[msb, id2], writes=[pcol])
            fw.op("dve", lambda e, l=l: e.tensor_copy(out=self.colmod[l][:].rearrange("p a b -> p (a b)"), in_=pcol[:, :]),
                  reads=[pcol], writes=[self.colmod[l]])
        fw.pop_scope()

    def phase_vec(self, l):
        fw = self.fw
        fw.push_scope()
        V = {}
        self.V = V
        cm = self.colmod[l]

        def colvec(name, src_ap, k):
            b = fw.sbuf(name, [128, k], F32)
            self.col_load("sp", b, b[:], src_ap.rearrange("(k p) -> p k", p=128))
            return b
        gn1 = colvec("gn1", self.g_norm1.t[l], 8)
        gn2 = colvec("gn2", self.g_norm2.t[l], 8)
        for nm, gn, i_sh, i_sc in (("1", gn1, 0, 1), ("2", gn2, 3, 4)):
            for seg in range(2):
                G = fw.sbuf("G%s_%d" % (nm, seg), [128, 8], F32)
                SH = fw.sbuf("SH%s_%d" % (nm, seg), [128, 8], F32)
                fw.op("dve", lambda e, G=G, i_sc=i_sc, seg=seg: e.tensor_scalar(out=G[:], in0=cm[:, i_sc * 8:(i_sc + 1) * 8, seg], scalar1=1.0, scalar2=32.0,
                                                                              op0=ALU.add, op1=ALU.mult), reads=[cm], writes=[G])
                fw.op("dve", lambda e, G=G, gn=gn: e.tensor_tensor(out=G[:], in0=G[:], in1=gn[:], op=ALU.mult), reads=[G, gn], writes=[G])
                fw.op("dve", lambda e, SH=SH, i_sh=i_sh, seg=seg: e.tensor_copy(out=SH[:], in_=cm[:, i_sh * 8:(i_sh + 1) * 8, seg]), reads=[cm], writes=[SH])
                V["G" + nm, seg] = G
                V["SH" + nm, seg] = SH
        cw = fw.sbuf("cw", [128, 4, 2], F32)
        for k in range(4):
            self.col_load("sp", cw, cw[:, k, :], self.conv_w.t[l, k].rearrange("(c p) -> p c", p=128))
        V["cw"] = cw
        V["cb"] = colvec("cb", self.conv_b.t[l], 2)
        for nm, src in (("ba", self.lru_b_a), ("bx", self.lru_b_x), ("lam", self.lru_lambda)):
            b = fw.sbuf(nm, [128, 2, 2], F32)
            for d in range(2):
                self.col_load("sp", b, b[:, d, :], src.t[l, d].rearrange("(c p) -> p c", p=128))
            V[nm] = b
        for nm in ("ba", "bx"):
            nb = fw.sbuf("n" + nm, [128, 2, 2], F32)
            fw.op("dve", lambda e, nb=nb, nm=nm: e.tensor_scalar(out=nb[:], in0=V[nm][:], scalar1=-1.0, scalar2=None, op0=ALU.mult), reads=[V[nm]], writes=[nb])
            V["n" + nm] = nb
        cA = fw.sbuf("cA", [128, 2, 2], F32)
        fw.op("act", lambda e: e.activation(out=cA[:], in_=V["lam"][:], func=AF.Exp, scale=-1.0), reads=[V["lam"]], writes=[cA])
        fw.op("act", lambda e: e.activation(out=cA[:], in_=cA[:], func=AF.Ln, bias=self.cst[:, 3:4], scale=1.0), reads=[cA, self.cst], writes=[cA])
        fw.op("dve", lambda e: e.tensor_scalar(out=cA[:], in0=cA[:], scalar1=-8.0, scalar2=None, op0=ALU.mult), reads=[cA], writes=[cA])
        V["cA"] = cA
        V["psc"] = colvec("psc", self.pool_scale.t[l], 2)
        V["gq"] = colvec("gq", self.g_q_lat.t[l], 3)
        V["gkv"] = colvec("gkv", self.g_kv_lat.t[l], 2)
        GQ = fw.sbuf("GQ", [128, 96], F32)
        GK = fw.sbuf("GK", [128, 96], F32)
        fw.dma("sp", GQ[:], self.g_qn.t[l].unsqueeze(0).to_broadcast([128, 96]), GQ, writes=[GQ])
        fw.dma("sp", GK[:], self.g_kn.t[l].unsqueeze(0).to_broadcast([128, 96]), GK, writes=[GK])
        fw.op("dve", lambda e: e.tensor_scalar(out=GK[:], in0=GK[:], scalar1=math.sqrt(96.0), scalar2=None, op0=ALU.mult), reads=[GK], writes=[GK])
        V["GQ"], V["GK"] = GQ, GK

    def end_vec(self):
        self.fw.pop_scope()

    def res_src(self, l, t0, n):
        if l == 0:
            if t0 < self.C:
                return self.ctx.t[t0:t0 + n, :]
            return self.x.t[t0 - self.C:t0 - self.C + n, :]
        return self.xres.t[t0:t0 + n, :]

    def norm_part(self, xts, nsub, junk, ss, rp, xn):
        fw = self.fw
        fw.op("dve", lambda e: e.memset(ss[:], 0.0), writes=[ss])
        for s in range(nsub):
            fw.op("act", lambda e, s=s: e.activation(out=junk[:], in_=xts[s][:], func=AF.Square, accum_out=ss[:, s:s + 1]),
                  reads=[xts[s]], writes=[junk, ss])
        fw.op("act", lambda e: e.activation(out=rp[:, 0:nsub], in_=ss[:, 0:nsub], func=AF.Sqrt, bias=self.cst[:, 0:1], scale=1.0),
              reads=[ss, self.cst], writes=[rp])
        fw.op("dve", lambda e: e.reciprocal(out=rp[:, 0:nsub], in_=rp[:, 0:nsub]), reads=[rp], writes=[rp])
        for s in range(nsub):
            fw.op("dve", lambda e, s=s: e.tensor_scalar(out=xn[s][:], in0=xts[s][:], scalar1=rp[:, s:s + 1], scalar2=None, op0=ALU.mult),
                  reads=[xts[s], rp], writes=[xn[s]])

    def transpose_part(self, xn, nsub, G, SH, hT, ptr_rot):
        fw = self.fw
        n = nsub * 128
        for cp in range(4):
            ptr = ptr_rot.next()
            for c2 in range(2):
                k = 2 * cp + c2
                for s in range(nsub):
                    fw.op("pe", lambda e, k=k, s=s, c2=c2, ptr=ptr: e.transpose(out=ptr[:, c2 * 512 + s * 128:c2 * 512 + (s + 1) * 128],
                                                                                in_=xn[s][:, k * 128:(k + 1) * 128], identity=self.idb[:]),
                          reads=[xn[s], self.idb], writes=[ptr])
            for c2 in range(2):
                k = 2 * cp + c2
                if c2 == 0:
                    fw.op("dve", lambda e, k=k, c2=c2, ptr=ptr: e.tensor_scalar(out=hT[:, k, 0:n], in0=ptr[:, c2 * 512:c2 * 512 + n], scalar1=G[:, k:k + 1],
                                                                                scalar2=SH[:, k:k + 1], op0=ALU.mult, op1=ALU.add),
                          reads=[ptr, G, SH], writes=[hT])
                else:
                    fw.op("act", lambda e, k=k, c2=c2, ptr=ptr: e.activation(out=hT[:, k, 0:n], in_=ptr[:, c2 * 512:c2 * 512 + n], func=AF.Identity,
                                                                             scale=G[:, k:k + 1], bias=SH[:, k:k + 1]),
                          reads=[ptr, G, SH], writes=[hT])

    def phase_a1(self, l):
        fw, V = self.fw, self.V
        fw.push_scope()
        win = fw.sbuf("win", [128, KD, DIN], BF16)
        for k in range(KD):
            fw.dma("pool", win[:, k, :], self.w_in.t[l, k * 128:(k + 1) * 128, :], win, writes=[win])
        xsets = [[fw.sbuf("xt%d_%d" % (i, s), [128, D], F32) for s in range(4)] for i in range(2)]
        xnS = [[fw.sbuf("xn%d_%d" % (i, s), [128, D], BF16) for s in range(4)] for i in range(2)]
        junk = fw.sbuf("junk", [128, D], BF16)
        ssS = [fw.sbuf("ss%d" % i, [128, 4], F32) for i in range(2)]
        rpS = [fw.sbuf("rp%d" % i, [128, 4], F32) for i in range(2)]
        hTS = [fw.sbuf("hT%d" % i, [128, KD, 512], BF16) for i in range(2)]
        ptr_rot = Rot([fw.psum("ptr%d" % i, [128, 1024], BF16) for i in range(2)])
        pz_rot = Rot([fw.psum("pz%d" % i, [128, 512], F32) for i in range(4)])
        stf = Rot([fw.sbuf("stf%d" % i, [128, 512], F32) for i in range(4)])
        stb = Rot([fw.sbuf("stb%d" % i, [128, 512], BF16) for i in range(8)])
        chunks = [(0, 128, "lx", 0), (128, 128, "lx", 128), (256, 128, "lg", 0), (384, 128, "lg", 128),
                  (512, 128, "q", 0), (640, 128, "q", 128), (768, 128, "q", 256), (896, 128, "q", 384), (1024, 128, "q", 512),
                  (1152, 32, "q", 640), (1184, 128, "pu", 0), (1312, 128, "pu", 128)]

        def load(gi):
            t0, n, seg = self.groups[gi]
            xs = xsets[gi % 2]
            rd = [self.p_xres[gi]] if l > 0 else []
            for s in range(n // 128):
                fw.dma("sp", xs[s][:], self.res_src(l, t0 + s * 128, 128), xs[s], reads=rd, writes=[xs[s]])

        NG = self.NG

        def stN(gi):
            t0, n, seg = self.groups[gi]
            self.norm_part(xsets[gi % 2], n // 128, junk, ssS[gi % 2], rpS[gi % 2], xnS[gi % 2])

        def stT(gi):
            t0, n, seg = self.groups[gi]
            self.transpose_part(xnS[gi % 2], n // 128, V["G1", seg], V["SH1", seg], hTS[gi % 2], ptr_rot)

        load(0)
        if NG > 1:
            load(1)
        stN(0)
        if NG > 2:
            load(2)
        if NG > 1:
            stN(1)
        stT(0)
        for gi, (t0, n, seg) in enumerate(self.groups):
            if gi + 2 < NG:
                stN(gi + 2)
            if gi + 3 < NG:
                load(gi + 3)
            if gi + 1 < NG:
                stT(gi + 1)
            nsub = n // 128
            hT = hTS[gi % 2]
            for ci, (c0, m, kind, r0) in enumerate(chunks):
                pz = pz_rot.next()
                for k in range(KD):
                    fw.op("pe", lambda e, k=k, c0=c0, m=m, pz=pz: e.matmul(pz[0:m, 0:n], lhsT=win[:, k, c0:c0 + m], rhs=hT[:, k, 0:n],
                                                                           start=(k == 0), stop=(k == KD - 1)), reads=[win, hT], writes=[pz])
                if kind == "lx":
                    st = stf.next()
                    fw.op("dve", lambda e, st=st, pz=pz: e.tensor_copy(out=st[:, 0:n], in_=pz[:, 0:n]), reads=[pz], writes=[st])
                    fw.dma("sp", self.zlx.t[r0:r0 + 128, t0:t0 + n], st[:, 0:n], st, reads=[st], writes=[self.p_zlx[gi]])
                elif kind == "pu":
                    st = stf.next()
                    fw.op("act", lambda e, st=st, pz=pz: e.activation(out=st[:, 0:n], in_=pz[:, 0:n], func=AF.Copy), reads=[pz], writes=[st])
                    fw.dma("sp", self.zpu.t[r0:r0 + 128, t0:t0 + n], st[:, 0:n], st, reads=[st], writes=[self.p_zpu[gi]])
                elif kind == "lg":
                    st = stb.next()
                    fw.op("act", lambda e, st=st, pz=pz: e.activation(out=st[:, 0:n], in_=pz[:, 0:n], func=AF.Gelu_apprx_tanh), reads=[pz], writes=[st])
                    fw.dma("sp", self.zgl.t[r0:r0 + 128, t0:t0 + n], st[:, 0:n], st, reads=[st], writes=[self.p_zgl[gi]])
                else:
                    st = stb.next()
                    eng = "dve" if ci % 2 == 0 else "act"
                    if eng == "dve":
                        fw.op("dve", lambda e, st=st, pz=pz, m=m: e.tensor_copy(out=st[0:m, 0:n], in_=pz[0:m, 0:n]), reads=[pz], writes=[st])
                    else:
                        fw.op("act", lambda e, st=st, pz=pz, m=m: e.activation(out=st[0:m, 0:n], in_=pz[0:m, 0:n], func=AF.Copy), reads=[pz], writes=[st])
                    fw.dma("sp", self.zq.t[r0:r0 + m, t0:t0 + n], st[0:m, 0:n], st, reads=[st], writes=[self.p_zq[gi]])
        fw.pop_scope()

    def phase_a2(self, l):
        fw, V = self.fw, self.V
        fw.push_scope()
        wuq = fw.sbuf("wuq", [128, 3, 768], BF16)
        wukv = fw.sbuf("wukv", [128, 2, 1024], BF16)
        wst = fw.sbuf("wst", [128, 1024], F32)
        for k in range(3):
            fw.dma("sp", wst[:, 0:768], self.w_uq.t[l, k * 128:(k + 1) * 128, :], wst, writes=[wst])
            fw.op("dve", lambda e, k=k: e.tensor_scalar(out=wuq[:, k, :], in0=wst[:, 0:768], scalar1=V["gq"][:, k:k + 1], scalar2=None, op0=ALU.mult),
                  reads=[wst, V["gq"]], writes=[wuq])
        for k in range(2):
            fw.dma("sp", wst[:], self.w_ukv.t[l, k * 128:(k + 1) * 128, :], wst, writes=[wst])
            fw.op("dve", lambda e, k=k: e.tensor_scalar(out=wukv[:, k, :], in0=wst[:], scalar1=V["gkv"][:, k:k + 1], scalar2=None, op0=ALU.mult),
                  reads=[wst, V["gkv"]], writes=[wukv])
        S96 = math.sqrt(96.0)
        gcol = fw.sbuf("gcol", [128, 2], F32)
        fw.op("dve", lambda e: e.memset(gcol[:], 1.0), writes=[gcol])
        self.col_load("sp", gcol, gcol[0:64, 0:1], self.g_qn.t[l, 0:64].unsqueeze(1))
        self.col_load("sp", gcol, gcol[0:64, 1:2], self.g_kn.t[l, 0:64].unsqueeze(1))
        fw.op("dve", lambda e: e.tensor_scalar(out=gcol[0:64, 1:2], in0=gcol[0:64, 1:2], scalar1=S96, scalar2=None, op0=ALU.mult), reads=[gcol], writes=[gcol])
        gr = {}
        for nm, src, mul in (("q", self.g_qn, 1.0), ("k", self.g_kn, S96)):
            g_c = fw.sbuf("grc" + nm, [128, 32], F32)
            g_s = fw.sbuf("grs" + nm, [128, 32], F32)
            fw.dma("sp", g_c[:], src.t[l, 64:96].unsqueeze(0).to_broadcast([128, 32]), g_c, writes=[g_c])
            for a_ in range(2):
                for b_ in range(2):
                    o = a_ * 16 + b_ * 8
                    o2 = 64 + a_ * 16 + (1 - b_) * 8
                    fw.dma("sp", g_s[:, o:o + 8], src.t[l, o2:o2 + 8].unsqueeze(0).to_broadcast([128, 8]), g_s, writes=[g_s])
            if mul != 1.0:
                fw.op("dve", lambda e, g_c=g_c: e.tensor_scalar(out=g_c[:], in0=g_c[:], scalar1=mul, scalar2=None, op0=ALU.mult), reads=[g_c], writes=[g_c])
                fw.op("dve", lambda e, g_s=g_s: e.tensor_scalar(out=g_s[:], in0=g_s[:], scalar1=mul, scalar2=None, op0=ALU.mult), reads=[g_s], writes=[g_s])
            gr[nm] = (g_c, g_s)
        zsets = [(fw.sbuf("zqa%d" % i, [128, 5, 512], BF16), fw.sbuf("zkr%d" % i, [32, 512], BF16),
                  fw.sbuf("rc%d" % i, [128, 4, 32], F32), fw.sbuf("rs%d" % i, [128, 4, 32], F32),
                  fw.sbuf("rcq%d" % i, [128, 4, 32], F32), fw.sbuf("rsq%d" % i, [128, 4, 32], F32),
                  fw.sbuf("rck%d" % i, [128, 4, 32], F32), fw.sbuf("rsk%d" % i, [128, 4, 32], F32)) for i in range(2)]
        sq = fw.sbuf("sq", [128, 5, 512], BF16)
        pss = fw.psum("pss", [128, 2], F32)
        pq = fw.psum("pq", [128, 1024], F32)
        pkv = fw.psum("pkv", [128, 1024], F32)
        pkr = fw.psum("pkr", [128, 1024], BF16)
        ptq = fw.psum("ptq", [128, 1024], BF16)
        ptk = fw.psum("ptk", [128, 1024], BF16)
        NB_ = 2
        mk = lambda nm, shape, dt: [fw.sbuf("%s%d" % (nm, i), shape, dt) for i in range(NB_)]
        r2 = mk("r2", [128, 2], F32)
        qs = mk("qs", [128, 8, 96], F32)
        ks = mk("ks", [128, 8, 96], F32)
        tmpq = mk("tmpq", [128, 8, 96], F32)
        tmpk = mk("tmpk", [128, 8, 96], F32)
        ssq = mk("ssq", [128, 8], F32)
        ssk = mk("ssk", [128, 8], F32)
        xrq = mk("xrq", [128, 8, 32], F32)
        xrk = mk("xrk", [128, 8, 32], F32)
        t1q = mk("t1q", [128, 8, 32], F32)
        t2q = mk("t2q", [128, 8, 32], F32)
        t1k = mk("t1k", [128, 8, 32], F32)
        t2k = mk("t2k", [128, 8, 32], F32)
        qb = mk("qb", [128, 8, 96], BF16)
        kb = mk("kb", [128, 8, 96], BF16)
        osets = [(fw.sbuf("qTst%d" % i, [96, 8, 512], BF16), fw.sbuf("kTst%d" % i, [96, 8, 512], BF16),
                  fw.sbuf("vst%d" % i, [128, 4, 512], BF16)) for i in range(2)]

        def load(gi):
            t0, n, seg = self.groups[gi]
            za, zk, rc, rs = zsets[gi % 2][0:4]
            fw.dma("sp", za[:, :, 0:n], self.zq.t[0:640, t0:t0 + n].rearrange("(k p) t -> p k t", p=128), za, reads=[self.p_zq[gi]], writes=[za])
            fw.dma("sp", zk[:, 0:n], self.zq.t[640:672, t0:t0 + n], zk, reads=[self.p_zq[gi]], writes=[zk])
            nsub = n // 128
            fw.dma("sp", rc[:, 0:nsub, :], self.ropeC.t[t0:t0 + n, :].rearrange("(s p) d -> p s d", p=128), rc, writes=[rc])
            fw.dma("sp", rs[:, 0:nsub, :], self.ropeS.t[t0:t0 + n, :].rearrange("(s p) d -> p s d", p=128), rs, writes=[rs])

        def group_pro(gi):
            t0, n, seg = self.groups[gi]
            nsub = n // 128
            za, zk, rc, rs, rcq, rsq, rck, rsk = zsets[gi % 2]
            fw.op("pool", lambda e: e.tensor_tensor(out=sq[:, :, 0:n], in0=za[:, :, 0:n], in1=za[:, :, 0:n], op=ALU.mult), reads=[za], writes=[sq])
            for dst, srcb, g_ in ((rcq, rc, gr["q"][0]), (rsq, rs, gr["q"][1]), (rck, rc, gr["k"][0]), (rsk, rs, gr["k"][1])):
                fw.op("pool", lambda e, dst=dst, srcb=srcb, g_=g_: e.tensor_tensor(out=dst[:, 0:nsub, :], in0=srcb[:, 0:nsub, :],
                                                                                 in1=g_[:].unsqueeze(1).to_broadcast([128, nsub, 32]), op=ALU.mult),
                      reads=[srcb, g_], writes=[dst])

        def stageA(idx):
            gi, s = items[idx]
            par = idx % NB_
            t0, n, seg = self.groups[gi]
            za, zk = zsets[gi % 2][0:2]
            vst = osets[gi % 2][2]
            sl = slice(s * 128, (s + 1) * 128)
            for k in range(3):
                fw.op("pe", lambda e, k=k: e.matmul(pss[:, 0:1], lhsT=sq[:, k, sl], rhs=self.onesb[:, 0:1], start=(k == 0), stop=(k == 2)),
                      reads=[sq, self.onesb], writes=[pss])
            for k in range(2):
                fw.op("pe", lambda e, k=k: e.matmul(pss[:, 1:2], lhsT=sq[:, 3 + k, sl], rhs=self.onesb[:, 0:1], start=(k == 0), stop=(k == 1)),
                      reads=[sq, self.onesb], writes=[pss])
            r2_ = r2[par]
            fw.op("act", lambda e: e.activation(out=r2_[:, 0:1], in_=pss[:, 0:1], func=AF.Sqrt, bias=self.cst[:, 1:2], scale=1.0 / 384.0),
                  reads=[pss, self.cst], writes=[r2_])
            fw.op("act", lambda e: e.activation(out=r2_[:, 1:2], in_=pss[:, 1:2], func=AF.Sqrt, bias=self.cst[:, 1:2], scale=1.0 / 256.0),
                  reads=[pss, self.cst], writes=[r2_])
            fw.op("dve", lambda e: e.reciprocal(out=r2_[:], in_=r2_[:]), reads=[r2_], writes=[r2_])
            for k in range(3):
                fw.op("pe", lambda e, k=k: e.matmul(pq[:, 0:512], lhsT=za[:, k, sl], rhs=wuq[:, k, 0:512], start=(k == 0), stop=(k == 2)),
                      reads=[za, wuq], writes=[pq])
            for k in range(3):
                fw.op("pe", lambda e, k=k: e.matmul(pq[:, 512:768], lhsT=za[:, k, sl], rhs=wuq[:, k, 512:768], start=(k == 0), stop=(k == 2)),
                      reads=[za, wuq], writes=[pq])
            for hf in range(2):
                for k in range(2):
                    fw.op("pe", lambda e, k=k, hf=hf: e.matmul(pkv[:, hf * 512:(hf + 1) * 512], lhsT=za[:, 3 + k, sl], rhs=wukv[:, k, hf * 512:(hf + 1) * 512],
                                                               start=(k == 0), stop=(k == 1)), reads=[za, wukv], writes=[pkv])
            fw.op("pe", lambda e: e.transpose(out=pkr[:, 0:32], in_=zk[0:32, sl], identity=self.idb[0:32, 0:32]), reads=[zk, self.idb], writes=[pkr])
            qs_, ks_ = qs[par], ks[par]
            fw.op("act", lambda e: e.activation(out=qs_[:].rearrange("p h d -> p (h d)"), in_=pq[:, 0:768], func=AF.Identity, scale=r2_[:, 0:1], bias=self.cst[:, 4:5]),
                  reads=[pq, r2_, self.cst], writes=[qs_])
            pkv3 = pkv[:, :].rearrange("p (h d) -> p h d", h=8)
            fw.op("act", lambda e: e.activation(out=ks_[:, :, 0:64], in_=pkv3[:, :, 0:64], func=AF.Identity, scale=r2_[:, 1:2], bias=self.cst[:, 4:5]),
                  reads=[pkv, r2_, self.cst], writes=[ks_])
            fw.op("dve", lambda e: e.tensor_copy(out=ks_[:, :, 64:96], in_=pkr[:, 0:32].unsqueeze(1).to_broadcast([128, 8, 32])), reads=[pkr], writes=[ks_])
            fw.op("act", lambda e: e.activation(out=vst[:, s, :].rearrange("p (h d) -> p h d", h=8), in_=pkv3[:, :, 64:128], func=AF.Identity,
                                                scale=r2_[:, 1:2], bias=self.cst[:, 4:5]), reads=[pkv, r2_, self.cst], writes=[vst])

        def chain_steps(src, tmp, ssx, xr, t1, t2, ob, rc_, rs_, s):
            xv = xr[:].rearrange("p h (a b f) -> p h a b f", a=2, b=2)
            t2v = t2[:].rearrange("p h (a b f) -> p h a b f", a=2, b=2)
            sv = rs_[:, s, :].rearrange("p (a b f) -> p a b f", a=2, b=2)
            return [
                lambda: fw.op("act", lambda e: e.activation(out=tmp[:], in_=src[:], func=AF.Square), reads=[src], writes=[tmp]),
                lambda: fw.op("dve", lambda e: e.tensor_reduce(out=ssx[:], in_=tmp[:], axis=AX.X, op=ALU.add), reads=[tmp], writes=[ssx]),
                lambda: fw.op("act", lambda e: e.activation(out=ssx[:], in_=ssx[:], func=AF.Sqrt, bias=self.cst[:, 2:3], scale=1.0), reads=[ssx, self.cst], writes=[ssx]),
                lambda: fw.op("dve", lambda e: e.reciprocal(out=ssx[:], in_=ssx[:]), reads=[ssx], writes=[ssx]),
                lambda: fw.op("dve", lambda e: e.tensor_tensor(out=ob[:, :, 0:64], in0=src[:, :, 0:64], in1=ssx[:].unsqueeze(2).to_broadcast([128, 8, 64]), op=ALU.mult),
                              reads=[src, ssx], writes=[ob]),
                lambda: fw.op("pool", lambda e: e.tensor_tensor(out=xr[:], in0=src[:, :, 64:96], in1=ssx[:].unsqueeze(2).to_broadcast([128, 8, 32]), op=ALU.mult),
                              reads=[src, ssx], writes=[xr]),
                lambda: fw.op("pool", lambda e: e.tensor_tensor(out=t1[:], in0=xr[:], in1=rc_[:, s, :].unsqueeze(1).to_broadcast([128, 8, 32]), op=ALU.mult),
                              reads=[xr, rc_], writes=[t1]),
                lambda: fw.op("pool", lambda e: e.tensor_tensor(out=t2v[:, :, :, 0, :], in0=xv[:, :, :, 1, :],
                                                                in1=sv[:, :, 0, :].unsqueeze(1).to_broadcast([128, 8, 2, 8]), op=ALU.mult), reads=[xr, rs_], writes=[t2]),
                lambda: fw.op("pool", lambda e: e.tensor_tensor(out=t2v[:, :, :, 1, :], in0=xv[:, :, :, 0, :],
                                                                in1=sv[:, :, 1, :].unsqueeze(1).to_broadcast([128, 8, 2, 8]), op=ALU.mult), reads=[xr, rs_], writes=[t2]),
                lambda: fw.op("pool", lambda e: e.tensor_tensor(out=ob[:, :, 64:96], in0=t1[:], in1=t2[:], op=ALU.add), reads=[t1, t2], writes=[ob]),
            ]

        def stageB(idx):
            gi, s = items[idx]
            par = idx % NB_
            rcq, rsq, rck, rsk = zsets[gi % 2][4:8]
            cq = chain_steps(qs[par], tmpq[par], ssq[par], xrq[par], t1q[par], t2q[par], qb[par], rcq, rsq, s)
            ck = chain_steps(ks[par], tmpk[par], ssk[par], xrk[par], t1k[par], t2k[par], kb[par], rck, rsk, s)
            for fq, fk in zip(cq, ck):
                fq()
                fk()

        def stageC(idx):
            gi, s = items[idx]
            par = idx % NB_
            t0, n, seg = self.groups[gi]
            qTst, kTst, vst = osets[gi % 2]
            sl = slice(s * 128, (s + 1) * 128)
            qb_, kb_ = qb[par], kb[par]
            for h in range(NH):
                fw.op("pe", lambda e, h=h: e.transpose(out=ptq[0:96, h * 128:(h + 1) * 128], in_=qb_[:, h, :], identity=self.idb[:]),
                      reads=[qb_, self.idb], writes=[ptq])
            for h in range(NH):
                fw.op("pe", lambda e, h=h: e.transpose(out=ptk[0:96, h * 128:(h + 1) * 128], in_=kb_[:, h, :], identity=self.idb[:]),
                      reads=[kb_, self.idb], writes=[ptk])
            fw.op("dve", lambda e: e.tensor_scalar(out=qTst[:, :, sl], in0=ptq[0:96, :].rearrange("p (h t) -> p h t", h=8), scalar1=gcol[0:96, 0:1], scalar2=None, op0=ALU.mult),
                  reads=[ptq, gcol], writes=[qTst])
            fw.op("act", lambda e: e.activation(out=kTst[:, :, sl], in_=ptk[0:96, :].rearrange("p (h t) -> p h t", h=8), func=AF.Identity,
                                                scale=gcol[0:96, 1:2], bias=self.cst[0:96, 4:5]), reads=[ptk, gcol, self.cst], writes=[kTst])
            if s == n // 128 - 1:
                nsub = n // 128
                fw.dma("sp", self.qT.t[:, :, t0:t0 + n].rearrange("h p t -> p h t"), qTst[:, :, 0:n], qTst, reads=[qTst], writes=[self.p_qT[gi]])
                fw.dma("sp", self.kT.t[:, :, t0:t0 + n].rearrange("h p t -> p h t"), kTst[:, :, 0:n], kTst, reads=[kTst], writes=[self.p_kT[gi]])
                fw.dma("sp", self.vv.t[t0:t0 + n, :].rearrange("(s p) d -> p s d", p=128), vst[:, 0:nsub, :], vst, reads=[vst], writes=[self.p_vv[gi]])

        items = [(gi, s) for gi, (t0, n, seg) in enumerate(self.groups) for s in range(n // 128)]
        NI = len(items)
        load(0)
        if self.NG > 1:
            load(1)
        group_pro(0)
        stageA(0)
        for idx in range(NI):
            gi, s = items[idx]
            if s == 0 and gi >= 1 and gi + 1 < self.NG:
                load(gi + 1)
            if idx + 1 < NI:
                if items[idx + 1][1] == 0:
                    group_pro(items[idx + 1][0])
                stageA(idx + 1)
            stageB(idx)
            if idx >= 1:
                stageC(idx - 1)
        stageC(NI - 1)
        fw.pop_scope()

    def seg_ranges(self):
        C, T = self.C, self.T
        return [((0, C), (PAD, PAD + C)), ((C, T), (2 * PAD + C, 2 * PAD + T))]

    def phase_b1(self, l):
        fw, V = self.fw, self.V
        T, TP = self.T, self.TP
        fw.push_scope()
        PU = fw.sbuf("PU", [128, 2, TP], F32)
        S2 = fw.sbuf("S2", [128, 2, TP], F32)
        S4 = fw.sbuf("S4", [128, 2, TP], F32)
        S8 = fw.sbuf("S8", [128, TP], F32)
        W = fw.sbuf("Wsel", [128, 2, TP], F32)
        M = fw.sbuf("Mx", [128, 2, T], BF16)
        WP = fw.sbuf("WP", [128, 2, 128], BF16)
        pc = fw.sbuf("pc", [128, 2, 2, 8], F32)
        piw = fw.sbuf("piw", [128, 2], F32)
        fw.dma("sp", pc[:], self.pcorr[:], pc, writes=[pc])
        fw.dma("sp", piw[:], self.pinvw[:], piw, writes=[piw])
        fw.op("pool", lambda e: e.memset(WP[:], 0.0), writes=[WP])
        for g in range(4):
            o = (g % 2) * 64
            fw.dma("pool", WP[o:o + 64, g // 2, o:o + 64], self.w_pool.t[l, g], WP, writes=[WP])
        fw.op("pool", lambda e: e.memset(PU[:], 0.0), writes=[PU])
        fw.op("dve", lambda e: e.memset(S2[:], 0.0), writes=[S2])
        fw.op("dve", lambda e: e.memset(S4[:], 0.0), writes=[S4])
        fw.op("dve", lambda e: e.memset(S8[:], 0.0), writes=[S8])
        fw.op("pool", lambda e: e.memset(W[:], 0.0), writes=[W])
        for (ta, tb), (pa, pb) in self.seg_ranges():
            fw.dma("sp", PU[:, :, pa:pb], self.zpu.t[:, ta:tb].rearrange("(c p) t -> p c t", p=128), PU, reads=self.p_zpu, writes=[PU])
        N = TP
        TT = lambda e, o, a, b: e.tensor_tensor(out=o, in0=a, in1=b, op=ALU.add)
        fw.op("pool", lambda e: TT(e, S2[:, :, 1:N], PU[:, :, 0:N - 1], PU[:, :, 1:N]), reads=[PU], writes=[S2])
        fw.op("dve", lambda e: TT(e, S4[:, :, 2:N - 1], S2[:, :, 1:N - 2], S2[:, :, 3:N]), reads=[S2], writes=[S4])
        fw.op("pool", lambda e: TT(e, S8[:, 4:N - 3], S4[:, 1, 2:N - 5], S4[:, 1, 6:N - 1]), reads=[S4], writes=[S8])
        fw.op("act", lambda e: e.activation(out=W[0:64, 0, :], in_=S2[0:64, 0, :], func=AF.Copy), reads=[S2], writes=[W])
        fw.op("act", lambda e: e.activation(out=W[64:128, 0, :], in_=S4[64:128, 0, :], func=AF.Copy), reads=[S4], writes=[W])
        fw.op("act", lambda e: e.activation(out=W[0:64, 1, :], in_=S8[0:64, :], func=AF.Copy), reads=[S8], writes=[W])
        fw.op("dve", lambda e: TT(e, W[64:128, 1, 8:N - 7], S8[64:128, 4:N - 11], S8[64:128, 12:N - 3]), reads=[S8], writes=[W])
        for (ta, tb), (pa, pb) in self.seg_ranges():
            fw.op("dve", lambda e, pa=pa: e.tensor_tensor(out=W[:, :, pa:pa + 8], in0=W[:, :, pa:pa + 8], in1=pc[:, :, 0, :], op=ALU.mult), reads=[W, pc], writes=[W])
            fw.op("dve", lambda e, pb=pb: e.tensor_tensor(out=W[:, :, pb - 8:pb], in0=W[:, :, pb - 8:pb], in1=pc[:, :, 1, :], op=ALU.mult), reads=[W, pc], writes=[W])
        for (ta, tb), (pa, pb) in self.seg_ranges():
            for c in range(2):
                fw.op("dve", lambda e, c=c, ta=ta, tb=tb, pa=pa, pb=pb: e.scalar_tensor_tensor(out=M[:, c, ta:tb], in0=W[:, c, pa:pb], scalar=piw[:, c:c + 1],
                                                                                            in1=PU[:, c, pa:pb], op0=ALU.mult, op1=ALU.subtract),
                      reads=[W, PU, piw], writes=[M])
        pp_rot = Rot([fw.psum("pp%d" % i, [128, 512], F32) for i in range(2)])
        st_rot = Rot([fw.sbuf("pst%d" % i, [128, 512], BF16) for i in range(3)])
        for gi, (t0, n, seg) in enumerate(self.groups):
            for c in range(2):
                pp = pp_rot.next()
                st = st_rot.next()
                fw.op("pe", lambda e, c=c, pp=pp: e.matmul(pp[:, 0:n], lhsT=WP[:, c, :], rhs=M[:, c, t0:t0 + n], start=True, stop=True), reads=[WP, M], writes=[pp])
                fw.op("act", lambda e, c=c, pp=pp, st=st: e.activation(out=st[:, 0:n], in_=pp[:, 0:n], func=AF.Identity, scale=V["psc"][:, c:c + 1], bias=self.cst[:, 4:5]),
                      reads=[pp, V["psc"], self.cst], writes=[st])
                fw.dma("sp", self.mixd.t[768 + c * 128:768 + (c + 1) * 128, t0:t0 + n], st[:, 0:n], st, reads=[st], writes=[self.p_mixC[gi]])
        fw.pop_scope()

    def phase_b2(self, l):
        fw, V = self.fw, self.V
        T, TP, C = self.T, self.TP, self.C
        fw.push_scope()
        WA = fw.sbuf("WA", [128, 2, 2, 128], BF16)
        WX = fw.sbuf("WX", [128, 2, 2, 128], BF16)
        fw.op("pool", lambda e: e.memset(WA[:], 0.0), writes=[WA])
        fw.op("pool", lambda e: e.memset(WX[:], 0.0), writes=[WX])
        for d in range(2):
            for h in range(4):
                o = (h % 2) * 64
                fw.dma("pool", WA[o:o + 64, d, h // 2, o:o + 64], self.lru_w_a.t[l, d, h], WA, writes=[WA])
                fw.dma("pool", WX[o:o + 64, d, h // 2, o:o + 64], self.lru_w_x.t[l, d, h], WX, writes=[WX])
        LX = fw.sbuf("LXp", [128, TP], F32)
        xc = fw.sbuf("xc", [128, TP], F32)
        xcb = fw.sbuf("xcb", [128, TP], BF16)
        RA = [fw.sbuf("RA%d" % d, [128, TP], F32) for d in range(2)]
        IB = [fw.sbuf("IB%d" % d, [128, TP], F32) for d in range(2)]
        A2 = [fw.sbuf("A2%d" % d, [128, TP], F32) for d in range(2)]
        H = [fw.sbuf("H%d" % d, [128, TP], F32) for d in range(2)]
        GL = fw.sbuf("GL", [128, T], BF16)
        AL = fw.sbuf("AL", [128, T], BF16)
        pg_rot = Rot([fw.psum("pg%d" % i, [128, 512], F32) for i in range(4)])
        segs = self.seg_ranges()
        pieces = [(p0, min(512, TP - p0)) for p0 in range(0, TP, 512)]
        (cta, ctb), (cpa, cpb) = segs[0]
        (lta, ltb), (lpa, lpb) = segs[1]
        rv = lambda b, a0, a1: b[:, a0:a1][:, ::-1]
        for c in range(2):
            fw.op("pool", lambda e: e.memset(LX[:], 0.0), writes=[LX])
            for (ta, tb), (pa, pb) in segs:
                fw.dma("sp", LX[:, pa:pb], self.zlx.t[c * 128:(c + 1) * 128, ta:tb], LX, reads=self.p_zlx, writes=[LX])
            fw.dma("sp", GL[:], self.zgl.t[c * 128:(c + 1) * 128, :], GL, reads=self.p_zgl, writes=[GL])
            cw, cb = V["cw"], V["cb"]
            N = TP
            fw.op("dve", lambda e: e.memset(xc[:], 0.0), writes=[xc])
            fw.op("dve", lambda e, c=c: e.tensor_scalar(out=xc[:, 1:N - 2], in0=LX[:, 0:N - 3], scalar1=cw[:, 0, c:c + 1], scalar2=cb[:, c:c + 1], op0=ALU.mult, op1=ALU.add),
                  reads=[LX, cw, cb], writes=[xc])
            for k in range(1, 4):
                fw.op("dve", lambda e, c=c, k=k: e.scalar_tensor_tensor(out=xc[:, 1:N - 2], in0=LX[:, k:N - 3 + k], scalar=cw[:, k, c:c + 1], in1=xc[:, 1:N - 2],
                                                                        op0=ALU.mult, op1=ALU.add), reads=[LX, cw, xc], writes=[xc])
            fw.op("act", lambda e: e.activation(out=xcb[:], in_=xc[:], func=AF.Copy), reads=[xc], writes=[xcb])
            for (p0, pn) in pieces:
                for d in range(2):
                    pa_ = pg_rot.next()
                    px_ = pg_rot.next()
                    fw.op("pe", lambda e: e.matmul(pa_[:, 0:pn], lhsT=WA[:, d, c, :], rhs=xcb[:, p0:p0 + pn], start=True, stop=True), reads=[WA, xcb], writes=[pa_])
                    fw.op("pe", lambda e: e.matmul(px_[:, 0:pn], lhsT=WX[:, d, c, :], rhs=xcb[:, p0:p0 + pn], start=True, stop=True), reads=[WX, xcb], writes=[px_])
                    fw.op("act", lambda e: e.activation(out=RA[d][:, p0:p0 + pn], in_=pa_[:, 0:pn], func=AF.Sigmoid, bias=V["ba"][:, d, c:c + 1], scale=1.0),
                          reads=[pa_, V["ba"]], writes=[RA[d]])
                    fw.op("act", lambda e: e.activation(out=IB[d][:, p0:p0 + pn], in_=px_[:, 0:pn], func=AF.Sigmoid, bias=V["bx"][:, d, c:c + 1], scale=1.0),
                          reads=[px_, V["bx"]], writes=[IB[d]])

            def dsteps(d):
                RA_, IB_, A2_, Hd = RA[d], IB[d], A2[d], H[d]
                st = [
                    lambda: fw.op("act", lambda e: e.activation(out=RA_[:], in_=RA_[:], func=AF.Exp, scale=V["cA"][:, d, c:c + 1], bias=self.cst[:, 4:5]),
                                  reads=[RA_, V["cA"], self.cst], writes=[RA_]),
                    lambda: fw.op("pool", lambda e: e.tensor_tensor(out=A2_[:], in0=RA_[:], in1=RA_[:], op=ALU.mult), reads=[RA_], writes=[A2_]),
                    lambda: fw.op("act", lambda e: e.activation(out=A2_[:], in_=A2_[:], func=AF.Sqrt, scale=-1.0, bias=self.cst[:, 3:4]), reads=[A2_, self.cst], writes=[A2_]),
                    lambda: fw.op("pool", lambda e: e.tensor_tensor(out=IB_[:], in0=IB_[:], in1=xc[:], op=ALU.mult), reads=[IB_, xc], writes=[IB_]),
                    lambda: fw.op("dve", lambda e: e.tensor_tensor(out=IB_[:], in0=IB_[:], in1=A2_[:], op=ALU.mult), reads=[IB_, A2_], writes=[IB_]),
                ]
                if d == 0:
                    st.append(lambda: fw.op("dve", lambda e: e.tensor_tensor_scan(out=Hd[:, cpa:cpb], data0=RA_[:, cpa:cpb], data1=IB_[:, cpa:cpb], initial=0.0,
                                                                                  op0=ALU.mult, op1=ALU.add), reads=[RA_, IB_], writes=[Hd]))
                    st.append(lambda: fw.op("dve", lambda e: e.tensor_tensor_scan(out=Hd[:, lpa:lpb], data0=RA_[:, lpa:lpb], data1=IB_[:, lpa:lpb], initial=Hd[:, cpb - 1:cpb],
                                                                                  op0=ALU.mult, op1=ALU.add), reads=[RA_, IB_, Hd], writes=[Hd]))
                else:
                    st.append(lambda: fw.op("dve", lambda e: e.tensor_tensor_scan(out=rv(Hd, cpa, cpb), data0=rv(RA_, cpa, cpb), data1=rv(IB_, cpa, cpb), initial=0.0,
                                                                                  op0=ALU.mult, op1=ALU.add), reads=[RA_, IB_], writes=[Hd]))
                    st.append(lambda: fw.op("dve", lambda e: e.tensor_tensor_scan(out=rv(Hd, lpa, lpb), data0=rv(RA_, lpa, lpb), data1=rv(IB_, lpa, lpb), initial=Hd[:, cpa:cpa + 1],
                                                                                  op0=ALU.mult, op1=ALU.add), reads=[RA_, IB_, Hd], writes=[Hd]))
                return st
            for f0, f1 in zip(dsteps(0), dsteps(1)):
                f0()
                f1()
            for (ta, tb), (pa, pb) in segs:
                fw.op("pool", lambda e, pa=pa, pb=pb: e.tensor_tensor(out=H[0][:, pa:pb], in0=H[0][:, pa:pb], in1=H[1][:, pa:pb], op=ALU.add), reads=[H[0], H[1]], writes=[H[0]])
                fw.op("dve", lambda e, ta=ta, tb=tb, pa=pa, pb=pb: e.tensor_tensor(out=AL[:, ta:tb], in0=H[0][:, pa:pb], in1=GL[:, ta:tb], op=ALU.mult),
                      reads=[H[0], GL], writes=[AL])
            fw.dma("sp", self.mixd.t[c * 128:(c + 1) * 128, :], AL[:], AL, reads=[AL], writes=self.p_mixA)
        fw.pop_scope()

    def gen_b(self, l, pst_rot):
        fw = self.fw
        TP = self.TP
        big = [fw.sbuf("bigw%d" % i, [128, TP], F32) for i in range(5)]
        xcb = fw.sbuf("xcb", [128, TP], BF16)
        yield from self.gen_b1(l, pst_rot, big, xcb)
        yield from self.gen_b2(l, pst_rot, big, xcb)

    def gen_b1(self, l, pst_rot, big, Mx):
        fw, V = self.fw, self.V
        T, TP = self.T, self.TP
        PU, S2, S4, S8, S16 = big
        WP = fw.sbuf("WP", [128, 2, 128], BF16)
        pc = fw.sbuf("pc", [128, 2, 2, 8], F32)
        piw = fw.sbuf("piw", [128, 2], F32)
        fw.dma("sp", pc[:], self.pcorr[:], pc, writes=[pc])
        fw.dma("sp", piw[:], self.pinvw[:], piw, writes=[piw])
        fw.op("pool", lambda e: e.memset(WP[:], 0.0), writes=[WP])
        for g in range(4):
            o = (g % 2) * 64
            fw.dma("pool", WP[o:o + 64, g // 2, o:o + 64], self.w_pool.t[l, g], WP, writes=[WP])
        st_rot = Rot([fw.sbuf("pst%d" % i, [128, 512], BF16) for i in range(3)])
        yield
        N = TP
        TT = lambda e, o, a, b: e.tensor_tensor(out=o, in0=a, in1=b, op=ALU.add)
        segs = self.seg_ranges()
        for c in range(2):
            fw.op("pool", lambda e: e.memset(PU[:], 0.0), writes=[PU])
            for (ta, tb), (pa, pb) in segs:
                fw.dma("sp", PU[:, pa:pb], self.zpu.t[c * 128:(c + 1) * 128, ta:tb], PU, reads=self.p_zpu, writes=[PU])
            yield
            fw.op("pool", lambda e: TT(e, S2[:, 1:N], PU[:, 0:N - 1], PU[:, 1:N]), reads=[PU], writes=[S2])
            yield
            fw.op("dve", lambda e: TT(e, S4[:, 2:N - 1], S2[:, 1:N - 2], S2[:, 3:N]), reads=[S2], writes=[S4])
            yield
            if c == 0:
                halves = [(0, 64, S2), (64, 128, S4)]
            else:
                fw.op("pool", lambda e: TT(e, S8[:, 4:N - 3], S4[:, 2:N - 5], S4[:, 6:N - 1]), reads=[S4], writes=[S8])
                yield
                fw.op("dve", lambda e: TT(e, S16[64:128, 8:N - 7], S8[64:128, 4:N - 11], S8[64:128, 12:N - 3]), reads=[S8], writes=[S16])
                yield
                halves = [(0, 64, S8), (64, 128, S16)]
            for (p0_, p1_, W) in halves:
                for (ta, tb), (pa, pb) in segs:
                    fw.op("dve", lambda e: e.tensor_tensor(out=W[p0_:p1_, pa:pa + 8], in0=W[p0_:p1_, pa:pa + 8], in1=pc[p0_:p1_, c, 0, :], op=ALU.mult), reads=[W, pc], writes=[W])
                    fw.op("dve", lambda e: e.tensor_tensor(out=W[p0_:p1_, pb - 8:pb], in0=W[p0_:p1_, pb - 8:pb], in1=pc[p0_:p1_, c, 1, :], op=ALU.mult), reads=[W, pc], writes=[W])
                    fw.op("dve", lambda e: e.scalar_tensor_tensor(out=Mx[p0_:p1_, ta:tb], in0=W[p0_:p1_, pa:pb], scalar=piw[p0_:p1_, c:c + 1],
                                                                  in1=PU[p0_:p1_, pa:pb], op0=ALU.mult, op1=ALU.subtract), reads=[W, PU, piw], writes=[Mx])
                    yield
            for gi, (t0, n, seg) in enumerate(self.groups):
                pp = pst_rot.next()
                st = st_rot.next()
                fw.op("pe", lambda e: e.matmul(pp[:, 0:n], lhsT=WP[:, c, :], rhs=Mx[:, t0:t0 + n], start=True, stop=True), reads=[WP, Mx], writes=[pp])
                fw.op("dve", lambda e: e.tensor_scalar(out=st[:, 0:n], in0=pp[:, 0:n], scalar1=V["psc"][:, c:c + 1], scalar2=None, op0=ALU.mult),
                      reads=[pp, V["psc"]], writes=[st])
                fw.dma("sp", self.mixd.t[768 + c * 128:768 + (c + 1) * 128, t0:t0 + n], st[:, 0:n], st, reads=[st], writes=[self.p_mixC[gi]])
                yield

    def gen_b2(self, l, pst_rot, big, xcb):
        fw, V = self.fw, self.V
        T, TP, C = self.T, self.TP, self.C
        WA = fw.sbuf("WA", [128, 2, 2, 128], BF16)
        WX = fw.sbuf("WX", [128, 2, 2, 128], BF16)
        fw.op("pool", lambda e: e.memset(WA[:], 0.0), writes=[WA])
        fw.op("pool", lambda e: e.memset(WX[:], 0.0), writes=[WX])
        for d in range(2):
            for h in range(4):
                o = (h % 2) * 64
                fw.dma("pool", WA[o:o + 64, d, h // 2, o:o + 64], self.lru_w_a.t[l, d, h], WA, writes=[WA])
                fw.dma("pool", WX[o:o + 64, d, h // 2, o:o + 64], self.lru_w_x.t[l, d, h], WX, writes=[WX])
        yield
        LX, xc, RA, IB, A2 = big
        H1 = fw.sbuf("H1", [128, TP], F32)
        GL = fw.sbuf("GL", [128, T], BF16)
        AL = fw.sbuf("AL", [128, T], BF16)
        segs = self.seg_ranges()
        pieces = [(p0, min(512, TP - p0)) for p0 in range(0, TP, 512)]
        (cta, ctb), (cpa, cpb) = segs[0]
        (lta, ltb), (lpa, lpb) = segs[1]
        rv = lambda b, a0, a1: b[:, a0:a1][:, ::-1]
        cw, cb = V["cw"], V["cb"]
        N = TP
        for c in range(2):
            fw.op("pool", lambda e: e.memset(LX[:], 0.0), writes=[LX])
            for (ta, tb), (pa, pb) in segs:
                fw.dma("sp", LX[:, pa:pb], self.zlx.t[c * 128:(c + 1) * 128, ta:tb], LX, reads=self.p_zlx, writes=[LX])
            fw.dma("sp", GL[:], self.zgl.t[c * 128:(c + 1) * 128, :], GL, reads=self.p_zgl, writes=[GL])
            yield
            fw.op("pool", lambda e: e.memset(xc[:], 0.0), writes=[xc])
            fw.op("dve", lambda e: e.tensor_scalar(out=xc[:, 1:N - 2], in0=LX[:, 0:N - 3], scalar1=cw[:, 0, c:c + 1], scalar2=cb[:, c:c + 1], op0=ALU.mult, op1=ALU.add),
                  reads=[LX, cw, cb], writes=[xc])
            yield
            for k in range(1, 4):
                fw.op("dve", lambda e: e.scalar_tensor_tensor(out=xc[:, 1:N - 2], in0=LX[:, k:N - 3 + k], scalar=cw[:, k, c:c + 1], in1=xc[:, 1:N - 2],
                                                              op0=ALU.mult, op1=ALU.add), reads=[LX, cw, xc], writes=[xc])
                yield
            fw.op("pool", lambda e: e.tensor_copy(out=xcb[:], in_=xc[:]), reads=[xc], writes=[xcb])
            yield
            for d in range(2):
                Hd = LX if d == 0 else H1
                for (p0, pn) in pieces:
                    pg = pst_rot.next()
                    fw.op("pe", lambda e: e.matmul(pg[:, 0:pn], lhsT=WA[:, d, c, :], rhs=xcb[:, p0:p0 + pn], start=True, stop=True), reads=[WA, xcb], writes=[pg])
                    fw.op("pe", lambda e: e.matmul(pg[:, 512:512 + pn], lhsT=WX[:, d, c, :], rhs=xcb[:, p0:p0 + pn], start=True, stop=True), reads=[WX, xcb], writes=[pg])
                    fw.op("act", lambda e: e.activation(out=RA[:, p0:p0 + pn], in_=pg[:, 0:pn], func=AF.Exp, bias=V["nba"][:, d, c:c + 1], scale=-1.0),
                          reads=[pg, V["nba"]], writes=[RA])
                    fw.op("act", lambda e: e.activation(out=IB[:, p0:p0 + pn], in_=pg[:, 512:512 + pn], func=AF.Exp, bias=V["nbx"][:, d, c:c + 1], scale=-1.0),
                          reads=[pg, V["nbx"]], writes=[IB])
                    yield
                    for buf in (RA, IB):
                        fw.op("dve", lambda e: e.tensor_scalar(out=buf[:, p0:p0 + pn], in0=buf[:, p0:p0 + pn], scalar1=1.0, scalar2=None, op0=ALU.add), reads=[buf], writes=[buf])
                        fw.op("dve", lambda e: e.reciprocal(out=buf[:, p0:p0 + pn], in_=buf[:, p0:p0 + pn]), reads=[buf], writes=[buf])
                    yield
                fw.op("act", lambda e: e.activation(out=RA[:], in_=RA[:], func=AF.Exp, scale=V["cA"][:, d, c:c + 1], bias=self.cst[:, 4:5]),
                      reads=[RA, V["cA"], self.cst], writes=[RA])
                yield
                fw.op("pool", lambda e: e.tensor_tensor(out=A2[:], in0=RA[:], in1=RA[:], op=ALU.mult), reads=[RA], writes=[A2])
                yield
                fw.op("act", lambda e: e.activation(out=A2[:], in_=A2[:], func=AF.Ln, scale=-1.0, bias=self.cst[:, 3:4]), reads=[A2, self.cst], writes=[A2])
                yield
                fw.op("act", lambda e: e.activation(out=A2[:], in_=A2[:], func=AF.Exp, scale=0.5, bias=self.cst[:, 4:5]), reads=[A2, self.cst], writes=[A2])
                yield
                fw.op("pool", lambda e: e.tensor_tensor(out=IB[:], in0=IB[:], in1=xc[:], op=ALU.mult), reads=[IB, xc], writes=[IB])
                yield
                fw.op("dve", lambda e: e.tensor_tensor(out=IB[:], in0=IB[:], in1=A2[:], op=ALU.mult), reads=[IB, A2], writes=[IB])
                yield
                if d == 0:
                    fw.op("dve", lambda e: e.tensor_tensor_scan(out=Hd[:, cpa:cpb], data0=RA[:, cpa:cpb], data1=IB[:, cpa:cpb], initial=0.0,
                                                                op0=ALU.mult, op1=ALU.add), reads=[RA, IB], writes=[Hd])
                    fw.op("dve", lambda e: e.tensor_tensor_scan(out=Hd[:, lpa:lpb], data0=RA[:, lpa:lpb], data1=IB[:, lpa:lpb], initial=Hd[:, cpb - 1:cpb],
                                                                op0=ALU.mult, op1=ALU.add), reads=[RA, IB, Hd], writes=[Hd])
                else:
                    fw.op("dve", lambda e: e.tensor_tensor_scan(out=rv(Hd, cpa, cpb), data0=rv(RA, cpa, cpb), data1=rv(IB, cpa, cpb), initial=0.0,
                                                                op0=ALU.mult, op1=ALU.add), reads=[RA, IB], writes=[Hd])
                    fw.op("dve", lambda e: e.tensor_tensor_scan(out=rv(Hd, lpa, lpb), data0=rv(RA, lpa, lpb), data1=rv(IB, lpa, lpb), initial=Hd[:, cpa:cpa + 1],
                                                                op0=ALU.mult, op1=ALU.add), reads=[RA, IB, Hd], writes=[Hd])
                yield
            for (ta, tb), (pa, pb) in segs:
                fw.op("pool", lambda e: e.tensor_tensor(out=LX[:, pa:pb], in0=LX[:, pa:pb], in1=H1[:, pa:pb], op=ALU.add), reads=[LX, H1], writes=[LX])
                fw.op("dve", lambda e: e.tensor_tensor(out=AL[:, ta:tb], in0=LX[:, pa:pb], in1=GL[:, ta:tb], op=ALU.mult), reads=[LX, GL], writes=[AL])
                yield
            fw.dma("sp", self.mixd.t[c * 128:(c + 1) * 128, :], AL[:], AL, reads=[AL], writes=self.p_mixA)
            yield

    def phase_c(self, l, last, bg=None):
        fw = self.fw
        T, C, L = self.T, self.C, self.L
        NKC = T // 128
        fw.push_scope()
        sets = []
        for par in range(2):
            kTh = fw.sbuf("kTh%d" % par, [128, T], BF16)
            qTh = fw.sbuf("qTh%d" % par, [128, T], BF16)
            fw.op("pool", lambda e, kTh=kTh: e.memset(kTh[:], 0.0), writes=[kTh])
            fw.op("pool", lambda e, qTh=qTh: e.memset(qTh[:], 0.0), writes=[qTh])
            Va = fw.sbuf("Va%d" % par, [128, NKC, 128], BF16)
            fw.op("pool", lambda e, Va=Va: e.memset(Va[:], 0.0), writes=[Va])
            oc = 64 if par == 0 else 0
            fw.op("pool", lambda e, Va=Va, oc=oc: e.memset(Va[:, :, oc:oc + 1], 1.0), writes=[Va])
            sets.append((kTh, qTh, Va))
        pst_rot = Rot([fw.psum("pst%d" % i, [128, 1024], F32) for i in range(3)])
        po_rot = Rot([fw.psum("po%d" % i, [128, 512], F32) for i in range(2)])
        pt_rot = Rot([fw.sbuf("pt%d" % i, [128, 1024], BF16) for i in range(4)])
        rden_rot = Rot([fw.sbuf("rden%d" % i, [128, 512], F32) for i in range(2)])
        bcs = fw.sbuf("bcs", [128, 512], F32)
        ast_rot = Rot([fw.sbuf("ast%d" % i, [128, 512], BF16) for i in range(3)])
        if bg:
            bg = self.gen_b(l, pst_rot)
            if DRAIN_FIRST:
                for _ in bg:
                    pass

        def load(h):
            kTh, qTh, Va = sets[h % 2]
            vo = 0 if h % 2 == 0 else 64
            fw.dma("sp", kTh[0:96, :], self.kT.t[h], kTh, reads=self.p_kT, writes=[kTh])
            fw.dma("sp", qTh[0:96, :], self.qT.t[h], qTh, reads=self.p_qT, writes=[qTh])
            for k0 in range(0, NKC, 4):
                k1 = min(NKC, k0 + 4)
                fw.dma("sp", Va[:, k0:k1, vo:vo + 64], self.vv.t[k0 * 128:k1 * 128, h * 64:(h + 1) * 64].rearrange("(kc p) d -> p kc d", p=128), Va,
                       reads=self.p_vv, writes=[Va])

        items = []
        for h in range(NH):
            for gi, (t0, n, seg) in enumerate(self.groups):
                if seg == 1 and last:
                    continue
                npairs = (C // 128 if seg == 1 else NKC) // 2
                for j in range(npairs):
                    items.append((h, gi, j, npairs))
        state = {"loaded": -1, "po": None, "defer": []}

        def ensure_loaded(h):
            while state["loaded"] < min(h, NH - 1):
                state["loaded"] += 1
                load(state["loaded"])

        def ST(it):
            h, gi, j, npairs = it
            ensure_loaded(h)
            kTh, qTh, Va = sets[h % 2]
            t0, n, seg = self.groups[gi]
            pst = pst_rot.next()
            pt = pt_rot.next()
            if FILL > 0:
                nf = min(FILL, n)
                fw.op("pe", lambda e: e.matmul(pst[:, 0:nf], lhsT=kTh[:, 2 * j * 128:(2 * j + 1) * 128], rhs=qTh[:, t0:t0 + nf], start=True, stop=True),
                      reads=[kTh, qTh], writes=[pst])
            for u in range(2):
                kc = 2 * j + u
                fw.op("pe", lambda e: e.matmul(pst[:, u * 512:u * 512 + n], lhsT=kTh[:, kc * 128:(kc + 1) * 128], rhs=qTh[:, t0:t0 + n], start=True, stop=True),
                      reads=[kTh, qTh], writes=[pst])
            v3 = lambda b: b[:, :].rearrange("p (u t) -> p u t", u=2)[:, :, 0:n]
            fw.op("act", lambda e: e.activation(out=v3(pt), in_=v3(pst), func=AF.Exp), reads=[pst], writes=[pt])
            return pt

        def PV(it, pt):
            h, gi, j, npairs = it
            kTh, qTh, Va = sets[h % 2]
            t0, n, seg = self.groups[gi]
            if j == 0:
                state["po"] = po_rot.next()
                state["rden"] = rden_rot.next()
                keep = []
                for ent in state["defer"]:
                    if ent[2] is state["po"]:
                        ent[1]()
                    else:
                        keep.append(ent)
                state["defer"] = keep
            po = state["po"]
            for u in range(2):
                kc = 2 * j + u
                fw.op("pe", lambda e: e.matmul(po[:, 0:n], lhsT=Va[:, kc, :], rhs=pt[:, u * 512:u * 512 + n], start=(kc == 0), stop=(kc == 2 * npairs - 1)),
                      reads=[Va, pt], writes=[po])
            if j == npairs - 1:
                dp = 64 if h % 2 == 0 else 0
                op_ = 0 if h % 2 == 0 else 64
                rden = state["rden"]

                fw.op("dve", lambda e: e.reciprocal(out=rden[dp:dp + 1, 0:n], in_=po[dp:dp + 1, 0:n]), reads=[po], writes=[rden])
                box = {}

                def epi2():
                    box["pbc"] = pst_rot.next()
                    fw.op("pe", lambda e: e.matmul(box["pbc"][:, 0:n], lhsT=self.onesf[dp:dp + 1, :], rhs=rden[dp:dp + 1, 0:n], start=True, stop=True),
                          reads=[self.onesf, rden], writes=[box["pbc"]])

                def epi3():
                    pbc = box["pbc"]
                    fw.op("dve", lambda e: e.tensor_copy(out=bcs[op_:op_ + 64, 0:n], in_=pbc[op_:op_ + 64, 0:n]), reads=[pbc], writes=[bcs])
                    ast = ast_rot.next()
                    fw.op("dve", lambda e: e.tensor_tensor(out=ast[op_:op_ + 64, 0:n], in0=po[op_:op_ + 64, 0:n], in1=bcs[op_:op_ + 64, 0:n], op=ALU.mult),
                          reads=[po, bcs], writes=[ast])
                    fw.dma("sp", self.mixd.t[256 + h * 64:256 + (h + 1) * 64, t0:t0 + n], ast[op_:op_ + 64, 0:n], ast, reads=[ast], writes=[self.p_mixB[gi]])
                state["defer"].append([5, lambda: (epi2(), epi3()), po])

        pts = {}
        LOOK = 2

        def run_deferred(force=False):
            keep = []
            for ent in state["defer"]:
                ent[0] -= 1
                if ent[0] <= 0 or force:
                    ent[1]()
                else:
                    keep.append(ent)
            state["defer"] = keep

        for i in range(min(LOOK, len(items))):
            pts[i] = ST(items[i])
        for i, it in enumerate(items):
            if i + LOOK < len(items):
                pts[i + LOOK] = ST(items[i + LOOK])
            PV(it, pts.pop(i))
            ensure_loaded(it[0] + 1)
            run_deferred()
            if bg and i % BG_STRIDE == 0:
                next(bg, None)
        while state["defer"]:
            run_deferred(force=True)
        if bg:
            for _ in bg:
                pass
        fw.pop_scope()

    def phase_d1(self, l, last):
        fw, V = self.fw, self.V
        fw.push_scope()
        WO = fw.sbuf("WO", [128, KD, D], BF16)
        for k in range(KD):
            fw.dma("pool", WO[:, k, :], self.w_out.t[l, k * 128:(k + 1) * 128, :], WO, writes=[WO])
        G1b = [fw.sbuf("G1b%d" % seg, [128, D], F32) for seg in range(2)]
        for seg in range(2):
            fw.dma("sp", G1b[seg][:], self.modv.t[l, seg, 2 * D:3 * D].unsqueeze(0).to_broadcast([128, D]), G1b[seg], reads=[self.p_modv[l]], writes=[G1b[seg]])
        msets = [fw.sbuf("mix%d" % i, [128, KD, 512], BF16) for i in range(2)]
        xsets = [[fw.sbuf("xt%d_%d" % (i, s), [128, D], F32) for s in range(4)] for i in range(2)]
        tmp_rot = Rot([fw.sbuf("tmp%d" % i, [128, D], F32) for i in range(2)])
        x1S = [[fw.sbuf("x1_%d_%d" % (i, s), [128, D], F32) for s in range(4)] for i in range(2)]
        xn = [fw.sbuf("xn%d" % s, [128, D], BF16) for s in range(4)]
        junk = fw.sbuf("junk", [128, D], BF16)
        ss = fw.sbuf("ss", [128, 4], F32)
        rp = fw.sbuf("rp", [128, 4], F32)
        hT_rot = Rot([fw.sbuf("h2T%d" % i, [128, KD, 512], BF16) for i in range(2)])
        py_rot = Rot([fw.psum("py%d" % i, [128, 1024], F32) for i in range(2)])
        ptr_rot = Rot([fw.psum("ptr%d" % i, [128, 1024], BF16) for i in range(2)])
        gl = [gi for gi, g in enumerate(self.groups) if not (last and g[2] == 1)]

        def load(ii):
            gi = gl[ii]
            t0, n, seg = self.groups[gi]
            mx = msets[ii % 2]
            fw.dma("sp", mx[:, :, 0:n], self.mixd.t[:, t0:t0 + n].rearrange("(k p) t -> p k t", p=128), mx,
                   reads=[self.p_mixA[gi], self.p_mixB[gi], self.p_mixC[gi]], writes=[mx])
            xs = xsets[ii % 2]
            rd = [self.p_xres[gi]] if l > 0 else []
            for s in range(n // 128):
                fw.dma("sp", xs[s][:], self.res_src(l, t0 + s * 128, 128), xs[s], reads=rd, writes=[xs[s]])

        def st1(ii):
            gi = gl[ii]
            t0, n, seg = self.groups[gi]
            mx, xs, x1 = msets[ii % 2], xsets[ii % 2], x1S[ii % 2]
            for s in range(n // 128):
                py = py_rot.next()
                for hf in range(2):
                    for k in range(KD):
                        fw.op("pe", lambda e, k=k, hf=hf, py=py, s=s: e.matmul(py[:, hf * 512:(hf + 1) * 512], lhsT=mx[:, k, s * 128:(s + 1) * 128],
                                                                              rhs=WO[:, k, hf * 512:(hf + 1) * 512], start=(k == 0), stop=(k == KD - 1)),
                              reads=[mx, WO], writes=[py])
                tmp = tmp_rot.next()
                fw.op("dve", lambda e, py=py, tmp=tmp: e.tensor_tensor(out=tmp[:], in0=py[:], in1=G1b[seg][:], op=ALU.mult), reads=[py, G1b[seg]], writes=[tmp])
                fw.op("pool", lambda e, s=s, tmp=tmp: e.tensor_tensor(out=x1[s][:], in0=xs[s][:], in1=tmp[:], op=ALU.add), reads=[xs[s], tmp], writes=[x1[s]])
                fw.dma("pool", self.x1d.t[t0 + s * 128:t0 + (s + 1) * 128, :], x1[s][:], x1[s], reads=[x1[s]], writes=[self.p_x1d[gi]])

        def stN(ii):
            gi = gl[ii]
            t0, n, seg = self.groups[gi]
            self.norm_part(x1S[ii % 2], n // 128, junk, ss, rp, xn)

        def stT(ii):
            gi = gl[ii]
            t0, n, seg = self.groups[gi]
            hT = hT_rot.next()
            self.transpose_part(xn, n // 128, V["G2", seg], V["SH2", seg], hT, ptr_rot)
            fw.dma("sp", self.h2d.t[:, t0:t0 + n].rearrange("(k p) t -> p k t", p=128), hT[:, :, 0:n], hT, reads=[hT], writes=[self.p_h2d[gi]])

        NGL = len(gl)
        load(0)
        if NGL > 1:
            load(1)
        st1(0)
        for ii in range(NGL):
            stN(ii)
            if ii + 1 < NGL:
                st1(ii + 1)
            if ii + 2 < NGL:
                load(ii + 2)
            stT(ii)
        fw.pop_scope()

    def phase_d2(self, l, last):
        fw, V = self.fw, self.V
        C = self.C
        fw.push_scope()
        W1 = fw.sbuf("W1", [128, KD, DFF], BF16)
        W2 = fw.sbuf("W2", [128, 32, D], BF16)
        for k in range(KD):
            fw.dma("pool", W1[:, k, :], self.w_ff1.t[l, k * 128:(k + 1) * 128, :], W1, writes=[W1])
        for kq in range(4):
            fw.dma("pool", W2[:, kq * 8:(kq + 1) * 8, :], self.w_ff2.t[l, kq * 1024:(kq + 1) * 1024, :].rearrange("(k p) n -> p k n", p=128), W2, writes=[W2])
        G2b = [fw.sbuf("G2b%d" % seg, [128, D], F32) for seg in range(2)]
        for seg in range(2):
            fw.dma("sp", G2b[seg][:], self.modv.t[l, seg, 5 * D:6 * D].unsqueeze(0).to_broadcast([128, D]), G2b[seg], reads=[self.p_modv[l]], writes=[G2b[seg]])
        hsets = [fw.sbuf("h2T%d" % i, [128, KD, 512], BF16) for i in range(2)]
        uT = fw.sbuf("uT", [128, 32, 512], BF16)
        rl_rot = Rot([fw.sbuf("rl%d" % i, [128, 512], F32) for i in range(3)])
        x1_rot = Rot([fw.sbuf("x1t%d" % i, [128, D], F32) for i in range(2)])
        tmp_rot = Rot([fw.sbuf("tmp%d" % i, [128, D], F32) for i in range(1)])
        pu_rot = Rot([fw.psum("pu%d" % i, [128, 512], F32) for i in range(4)])
        py_rot = Rot([fw.psum("py%d" % i, [128, 1024], F32) for i in range(2)])
        gl = [gi for gi, g in enumerate(self.groups) if not (last and g[2] == 1)]

        def load(gi):
            t0, n, seg = self.groups[gi]
            fw.dma("sp", hsets[gi % 2][:, :, 0:n], self.h2d.t[:, t0:t0 + n].rearrange("(k p) t -> p k t", p=128), hsets[gi % 2], reads=[self.p_h2d[gi]], writes=[hsets[gi % 2]])

        load(gl[0])
        for ii, gi in enumerate(gl):
            t0, n, seg = self.groups[gi]
            if ii + 1 < len(gl):
                load(gl[ii + 1])
            nsub = n // 128
            hT = hsets[gi % 2]
            for oc in range(32):
                pu = pu_rot.next()
                rl = rl_rot.next()
                for k in range(KD):
                    fw.op("pe", lambda e, k=k, oc=oc, pu=pu: e.matmul(pu[:, 0:n], lhsT=W1[:, k, oc * 128:(oc + 1) * 128], rhs=hT[:, k, 0:n], start=(k == 0), stop=(k == KD - 1)),
                          reads=[W1, hT], writes=[pu])
                if oc % 2 == 0:
                    fw.op("act", lambda e, pu=pu, rl=rl: e.activation(out=rl[:, 0:n], in_=pu[:, 0:n], func=AF.Relu), reads=[pu], writes=[rl])
                    fw.op("dve", lambda e, oc=oc, rl=rl: e.tensor_tensor(out=uT[:, oc, 0:n], in0=rl[:, 0:n], in1=rl[:, 0:n], op=ALU.mult), reads=[rl], writes=[uT])
                else:
                    fw.op("dve", lambda e, pu=pu, rl=rl: e.tensor_scalar(out=rl[:, 0:n], in0=pu[:, 0:n], scalar1=0.0, scalar2=None, op0=ALU.max), reads=[pu], writes=[rl])
                    fw.op("pool", lambda e, oc=oc, rl=rl: e.tensor_tensor(out=uT[:, oc, 0:n], in0=rl[:, 0:n], in1=rl[:, 0:n], op=ALU.mult), reads=[rl], writes=[uT])
            for s in range(nsub):
                x1t = x1_rot.next()
                fw.dma("sp", x1t[:], self.x1d.t[t0 + s * 128:t0 + (s + 1) * 128, :], x1t, reads=[self.p_x1d[gi]], writes=[x1t])
                py = py_rot.next()
                for hf in range(2):
                    for kc in range(32):
                        fw.op("pe", lambda e, kc=kc, hf=hf, py=py, s=s: e.matmul(py[:, hf * 512:(hf + 1) * 512], lhsT=uT[:, kc, s * 128:(s + 1) * 128],
                                                                                rhs=W2[:, kc, hf * 512:(hf + 1) * 512], start=(kc == 0), stop=(kc == 31)),
                              reads=[uT, W2], writes=[py])
                tmp = tmp_rot.next()
                fw.op("dve", lambda e, py=py, tmp=tmp: e.tensor_tensor(out=tmp[:], in0=py[:], in1=G2b[seg][:], op=ALU.mult), reads=[py, G2b[seg]], writes=[tmp])
                fw.op("pool", lambda e, x1t=x1t, tmp=tmp: e.tensor_tensor(out=x1t[:], in0=x1t[:], in1=tmp[:], op=ALU.add), reads=[x1t, tmp], writes=[x1t])
                r0 = t0 + s * 128
                if last:
                    fw.dma("sp", self.out.t[r0 - C:r0 - C + 128, :], x1t[:], x1t, reads=[x1t], writes=[self.p_out[gi]])
                else:
                    fw.dma("sp", self.xres.t[r0:r0 + 128, :], x1t[:], x1t, reads=[x1t], writes=[self.p_xres[gi]])
        fw.pop_scope()


def const_tables(L, C):
    T = L + C
    ropeC = np.ones((T, 32), np.float32)
    ropeS = np.zeros((T, 32), np.float32)
    t = np.arange(L)
    pos = np.stack([(t // GRID_W).astype(np.float32), (t % GRID_W).astype(np.float32)], axis=-1)
    freqs = np.power(np.float32(10000.0), -np.arange(8, dtype=np.float32) / np.float32(8)).astype(np.float32)
    ang = (pos[:, :, None] * freqs).astype(np.float32)
    cs, sn = np.cos(ang).astype(np.float32), np.sin(ang).astype(np.float32)
    cb = np.stack([cs, cs], axis=2)
    ssg = np.stack([-sn, sn], axis=2)
    ropeC[C:] = cb.reshape(L, 32)
    ropeS[C:] = ssg.reshape(L, 32)
    wins = {(0, 0): 2, (0, 1): 4, (1, 0): 8, (1, 1): 16}
    pcorr = np.ones((128, 2, 2, 8), np.float32)
    pinvw = np.ones((128, 2), np.float32)
    for (c, hf), w in wins.items():
        ps = slice(hf * 64, (hf + 1) * 64)
        pinvw[ps, c] = 1.0 / w
        for j in range(8):
            cntl = (j + w // 2) - max(j - w // 2, 0)
            pcorr[ps, c, 0, j] = w / cntl
            tt = j - 8
            cntr = min(tt + w // 2, 0) - (tt - w // 2)
            pcorr[ps, c, 1, j] = w / cntr
    ident = np.eye(128, dtype=np.float32).astype(ml_dtypes.bfloat16)
    return dict(ropeC=ropeC, ropeS=ropeS, pcorr=pcorr, pinvw=pinvw, ident=ident)


_W_NAMES = ["w_mod", "b_mod", "g_norm1", "g_norm2", "w_in", "conv_w", "conv_b", "lru_w_a", "lru_b_a", "lru_w_x", "lru_b_x",
            "lru_lambda", "g_q_lat", "w_uq", "g_kv_lat", "w_ukv", "g_qn", "g_kn", "w_pool", "pool_scale", "w_out", "w_ff1", "w_ff2"]


def make_in_maps(inputs, L, C, NL):
    tabs = const_tables(L, C)
    B = inputs["x"].shape[0]
    shared = {k: np.ascontiguousarray(np.asarray(inputs[k], np.float32)[:NL]) for k in _W_NAMES}
    shared.update(tabs)
    maps = []
    for b in range(B):
        m = dict(shared)
        m["x"] = np.ascontiguousarray(inputs["x"][b], dtype=np.float32)
        m["ctx"] = np.ascontiguousarray(inputs["ctx"][b], dtype=np.float32)
        m["cvec"] = np.ascontiguousarray(np.stack([inputs["c"][b], inputs["c_ctx"]], axis=0), dtype=np.float32)
        maps.append(m)
    return maps


_CACHE = {}


def kernel(**inputs):
    x = np.asarray(inputs["x"])
    B, L, _ = x.shape
    C = np.asarray(inputs["ctx"]).shape[1]
    NL = np.asarray(inputs["w_in"]).shape[0]
    key = (L, C, NL)
    if key not in _CACHE:
        _CACHE[key] = Prog(L, C, NL).build()
    nc = _CACHE[key]
    inputs = {k: np.asarray(v) for k, v in inputs.items()}
    maps = make_in_maps(inputs, L, C, NL)
    res = run_bass_kernel_spmd(nc, maps, core_ids=list(range(B)))
    return np.stack([np.asarray(r["out"], dtype=np.float32) for r in res.results], axis=0)
```
